# Optimizing a Trainium2 kernel written in Bass

```python
import jax, jax.numpy as jnp
from jax import lax
import numpy as np

D_MODEL = 1024
BATCH = 4
SEQ = 4096
DEPTH = 4
DEC_BATCH = 128
DEC_SEQ = 8
PAST_LEN = 8192
PAGE_SIZE = 128

D_CONV = D_MODEL // 2
CONV_WIDTH = 31
D_GMLP = D_MODEL // 2
GMLP_GROUPS = 8
GMLP_GROUP_DIM = D_GMLP // GMLP_GROUPS
GMLP_CHUNK = 128
N_HEADS = 8
N_KV_HEADS = 2
HEAD_DIM = 64
Q_PER_KV = N_HEADS // N_KV_HEADS
WINDOW = 128
ATTN_BLOCK = WINDOW
ROPE_THETA = 10000.0
D_FF = ((8 * D_MODEL // 3 + 127) // 128) * 128
N_BRANCH = 3
EPS = 1e-6
NEG_INF = -1e30
C_GLU = 2 * D_CONV
C_GMLP = 2 * D_GMLP
C_Q = N_HEADS * HEAD_DIM
C_KV = N_KV_HEADS * HEAD_DIM
C_GATE = N_BRANCH * D_MODEL
D_IN = C_GLU + C_GMLP + C_Q + 2 * C_KV + C_GATE
SPLITS = (C_GLU, C_GLU + C_GMLP, C_GLU + C_GMLP + C_Q, C_GLU + C_GMLP + C_Q + C_KV, C_GLU + C_GMLP + C_Q + 2 * C_KV)

kernel_name = 'hybrid_conv_gmlp_swa_decoder_step'


def rms_norm(x, g):
    xf = x.astype(jnp.float32)
    y = xf * lax.rsqrt(jnp.mean(xf * xf, axis=-1, keepdims=True) + EPS)
    return y.astype(x.dtype) * g


def layer_norm(x, g, b):
    xf = x.astype(jnp.float32)
    mu = jnp.mean(xf, axis=-1, keepdims=True)
    xc = xf - mu
    y = xc * lax.rsqrt(jnp.mean(xc * xc, axis=-1, keepdims=True) + EPS)
    return y.astype(x.dtype) * g + b


def swiglu(x, w_gu, w_down):
    gu = x @ w_gu
    return (jax.nn.silu(gu[..., :D_FF]) * gu[..., D_FF:]) @ w_down


def rope(x, pos):
    half = HEAD_DIM // 2
    inv_freq = 1.0 / (ROPE_THETA ** (jnp.arange(half, dtype=jnp.float32) / half))
    ang = pos.astype(jnp.float32)[:, None] * inv_freq[None, :]
    cos = jnp.cos(ang)[None, :, None, :]
    sin = jnp.sin(ang)[None, :, None, :]
    xf = x.astype(jnp.float32)
    x1, x2 = xf[..., :half], xf[..., half:]
    return jnp.concatenate([x1 * cos - x2 * sin, x2 * cos + x1 * sin], axis=-1).astype(x.dtype)


def sink_attention(q, k, v, mask, sinks):
    s = jnp.einsum('...qkgd,...skd->...kgqs', q, k).astype(jnp.float32) * (HEAD_DIM ** -0.5)
    s = jnp.where(mask, s, NEG_INF)
    sk = sinks.astype(jnp.float32).reshape(N_KV_HEADS, Q_PER_KV, 1)
    m = jnp.maximum(jnp.max(s, axis=-1), sk)
    p = jnp.exp(s - m[..., None])
    denom = jnp.sum(p, axis=-1) + jnp.exp(sk - m)
    p = (p / denom[..., None]).astype(v.dtype)
    return jnp.einsum('...kgqs,...skd->...qkgd', p, v)


def attn_prompt(q, k, v, sinks):
    B, L = q.shape[0], q.shape[1]
    nb = L // ATTN_BLOCK
    qb = q.reshape(B, nb, ATTN_BLOCK, N_KV_HEADS, Q_PER_KV, HEAD_DIM)
    kb = k.reshape(B, nb, ATTN_BLOCK, N_KV_HEADS, HEAD_DIM)
    vb = v.reshape(B, nb, ATTN_BLOCK, N_KV_HEADS, HEAD_DIM)
    pad = ((0, 0), (1, 0), (0, 0), (0, 0), (0, 0))
    kk = jnp.concatenate([jnp.pad(kb[:, :-1], pad), kb], axis=2)
    vv = jnp.concatenate([jnp.pad(vb[:, :-1], pad), vb], axis=2)
    qi = jnp.arange(ATTN_BLOCK)[:, None]
    si = jnp.arange(2 * ATTN_BLOCK)[None, :]
    delta = qi + ATTN_BLOCK - si
    band = (delta >= 0) & (delta <= WINDOW)
    blk = jnp.arange(nb)[:, None, None]
    valid = band[None] & ((si[None] >= ATTN_BLOCK) | (blk >= 1))
    o = sink_attention(qb, kk, vv, valid[None, :, None, None], sinks)
    return o.reshape(B, L, N_HEADS * HEAD_DIM)


def attn_sample(q, k_new, v_new, k_past, v_past, sinks):
    B, T = q.shape[0], q.shape[1]
    lb = k_past.shape[1]
    kk = jnp.concatenate([k_past.astype(k_new.dtype), k_new], axis=1)
    vv = jnp.concatenate([v_past.astype(v_new.dtype), v_new], axis=1)
    delta = jnp.arange(T)[:, None] + lb - jnp.arange(lb + T)[None, :]
    mask = (delta >= 0) & (delta <= WINDOW)
    o = sink_attention(q.reshape(B, T, N_KV_HEADS, Q_PER_KV, HEAD_DIM), kk, vv, mask, sinks)
    return o.reshape(B, T, N_HEADS * HEAD_DIM)


def token_mixer(h, pos, conv_prev, k_past, v_past, is_prompt, w_in, w_conv_dw, b_conv_dw, ln_conv_g, ln_conv_b,
                w_conv_out, ln_gmlp_g, ln_gmlp_b, w_spatial, b_spatial, w_gmlp_out, q_norm_g, k_norm_g, sinks, w_o, w_out):
    B, L = h.shape[0], h.shape[1]
    z = h @ w_in
    a_in, g_in, q, k, v, gates = jnp.split(z, SPLITS, axis=-1)
    a = a_in[..., :D_CONV] * jax.nn.sigmoid(a_in[..., D_CONV:])
    xp = jnp.concatenate([conv_prev.astype(a.dtype), a], axis=1)
    c = lax.conv_general_dilated(xp, w_conv_dw[:, None, :].astype(xp.dtype), (1,), 'VALID',
                                 dimension_numbers=('NWC', 'WIO', 'NWC'), feature_group_count=D_CONV) + b_conv_dw
    y_a = jax.nn.silu(layer_norm(c, ln_conv_g, ln_conv_b)) @ w_conv_out
    conv_state = xp[:, -(CONV_WIDTH - 1):]
    g_in = jax.nn.gelu(g_in)
    u = g_in[..., :D_GMLP]
    vg = layer_norm(g_in[..., D_GMLP:], ln_gmlp_g, ln_gmlp_b)
    n = min(L, GMLP_CHUNK)
    vc = vg.reshape(B, L // n, n, GMLP_GROUPS, GMLP_GROUP_DIM)
    causal = jnp.arange(n)[:, None] >= jnp.arange(n)[None, :]
    w_s = jnp.where(causal, w_spatial[:, :n, :n], 0.0)
    sg = jnp.einsum('gts,bcsgd->bctgd', w_s, vc) + b_spatial[:, :n].T[None, None, :, :, None]
    y_b = (u * sg.reshape(B, L, D_GMLP)) @ w_gmlp_out
    q = rope(rms_norm(q.reshape(B, L, N_HEADS, HEAD_DIM), q_norm_g), pos)
    k = rope(rms_norm(k.reshape(B, L, N_KV_HEADS, HEAD_DIM), k_norm_g), pos)
    v = v.reshape(B, L, N_KV_HEADS, HEAD_DIM)
    if is_prompt:
        o = attn_prompt(q, k, v, sinks)
        k_rows, v_rows = k[:, -WINDOW:], v[:, -WINDOW:]
    else:
        o = attn_sample(q, k, v, k_past, v_past, sinks)
        k_rows, v_rows = k, v
    y_c = o @ w_o
    gt = jax.nn.sigmoid(gates).reshape(B, L, N_BRANCH, D_MODEL)
    mix = gt[..., 0, :] * y_a + gt[..., 1, :] * y_b + gt[..., 2, :] * y_c
    return mix @ w_out, conv_state, k_rows, v_rows, vg


def decoder_layer(x, pos, conv_prev, k_past, v_past, is_prompt, ffn_w, mixer_w):
    n1, gu1, dn1, nm, n2, gu2, dn2 = ffn_w
    x = x + 0.5 * swiglu(rms_norm(x, n1), gu1, dn1)
    m, conv_state, k_rows, v_rows, v_gmlp = token_mixer(rms_norm(x, nm), pos, conv_prev, k_past, v_past, is_prompt, *mixer_w)
    x = x + m
    x = x + 0.5 * swiglu(rms_norm(x, n2), gu2, dn2)
    return x, conv_state, k_rows, v_rows, v_gmlp


def setup_inputs(seed: int = 0) -> dict:
    key = jax.random.key(seed)
    ks = jax.random.split(key, 28)

    def nrm(k, shape, scale):
        return jax.random.normal(k, shape, jnp.float32) * scale

    def gain(k, shape):
        return 1.0 + 0.02 * jax.random.normal(k, shape, jnp.float32)

    win_buf = min(WINDOW, PAST_LEN)
    L = DEPTH
    return {
        'x_prompt': nrm(ks[0], (BATCH, SEQ, D_MODEL), 1.0),
        'x_sample': nrm(ks[1], (DEC_BATCH, DEC_SEQ, D_MODEL), 1.0),
        'state_conv': nrm(ks[2], (L, DEC_BATCH, CONV_WIDTH - 1, D_CONV), 0.5),
        'cache_k': nrm(ks[3], (L, DEC_BATCH, win_buf, N_KV_HEADS, HEAD_DIM), 1.0),
        'cache_v': nrm(ks[4], (L, DEC_BATCH, win_buf, N_KV_HEADS, HEAD_DIM), 1.0),
        'norm_ffn1': gain(ks[5], (L, D_MODEL)),
        'w_ffn1_gu': nrm(ks[6], (L, D_MODEL, 2 * D_FF), D_MODEL ** -0.5),
        'w_ffn1_down': nrm(ks[7], (L, D_FF, D_MODEL), D_FF ** -0.5),
        'norm_mix': gain(ks[8], (L, D_MODEL)),
        'w_in': nrm(ks[9], (L, D_MODEL, D_IN), D_MODEL ** -0.5),
        'w_conv_dw': nrm(ks[10], (L, CONV_WIDTH, D_CONV), CONV_WIDTH ** -0.5),
        'b_conv_dw': nrm(ks[11], (L, D_CONV), 0.02),
        'ln_conv_g': gain(ks[12], (L, D_CONV)),
        'ln_conv_b': nrm(ks[13], (L, D_CONV), 0.02),
        'w_conv_out': nrm(ks[14], (L, D_CONV, D_MODEL), D_CONV ** -0.5),
        'ln_gmlp_g': gain(ks[15], (L, D_GMLP)),
        'ln_gmlp_b': nrm(ks[16], (L, D_GMLP), 0.02),
        'w_spatial': nrm(ks[17], (L, GMLP_GROUPS, GMLP_CHUNK, GMLP_CHUNK), GMLP_CHUNK ** -0.5),
        'b_spatial': gain(ks[18], (L, GMLP_GROUPS, GMLP_CHUNK)),
        'w_gmlp_out': nrm(ks[19], (L, D_GMLP, D_MODEL), D_GMLP ** -0.5),
        'q_norm_g': gain(ks[20], (L, HEAD_DIM)),
        'k_norm_g': gain(ks[21], (L, HEAD_DIM)),
        'attn_sinks': nrm(ks[22], (L, N_HEADS), 1.0),
        'w_o': nrm(ks[23], (L, C_Q, D_MODEL), C_Q ** -0.5),
        'w_out': nrm(ks[24], (L, D_MODEL, D_MODEL), D_MODEL ** -0.5),
        'norm_ffn2': gain(ks[25], (L, D_MODEL)),
        'w_ffn2_gu': nrm(ks[26], (L, D_MODEL, 2 * D_FF), D_MODEL ** -0.5),
        'w_ffn2_down': nrm(ks[27], (L, D_FF, D_MODEL), D_FF ** -0.5),
    }


def reference(x_prompt, x_sample, state_conv, cache_k, cache_v, norm_ffn1, w_ffn1_gu, w_ffn1_down, norm_mix, w_in,
              w_conv_dw, b_conv_dw, ln_conv_g, ln_conv_b, w_conv_out, ln_gmlp_g, ln_gmlp_b, w_spatial, b_spatial,
              w_gmlp_out, q_norm_g, k_norm_g, attn_sinks, w_o, w_out, norm_ffn2, w_ffn2_gu, w_ffn2_down):
    bp, lp_len = x_prompt.shape[0], x_prompt.shape[1]
    pos_p = jnp.arange(lp_len, dtype=jnp.int32)
    pos_s = PAST_LEN + jnp.arange(x_sample.shape[1], dtype=jnp.int32)
    conv_zero = jnp.zeros((bp, CONV_WIDTH - 1, D_CONV), x_prompt.dtype)
    hp, hs = x_prompt, x_sample
    conv_p, conv_s, k_p, v_p, k_s, v_s, gv_s = [], [], [], [], [], [], []
    for l in range(DEPTH):
        ffn_w = (norm_ffn1[l], w_ffn1_gu[l], w_ffn1_down[l], norm_mix[l], norm_ffn2[l], w_ffn2_gu[l], w_ffn2_down[l])
        mixer_w = (w_in[l], w_conv_dw[l], b_conv_dw[l], ln_conv_g[l], ln_conv_b[l], w_conv_out[l], ln_gmlp_g[l],
                   ln_gmlp_b[l], w_spatial[l], b_spatial[l], w_gmlp_out[l], q_norm_g[l], k_norm_g[l], attn_sinks[l],
                   w_o[l], w_out[l])
        hp, cp, kp, vp, _ = decoder_layer(hp, pos_p, conv_zero, None, None, True, ffn_w, mixer_w)
        hs, cs, ks_, vs_, gv = decoder_layer(hs, pos_s, state_conv[l], cache_k[l], cache_v[l], False, ffn_w, mixer_w)
        conv_p.append(cp); conv_s.append(cs)
        k_p.append(kp); v_p.append(vp)
        k_s.append(ks_); v_s.append(vs_)
        gv_s.append(gv)
    return (hp, hs, jnp.stack(conv_p), jnp.stack(conv_s), jnp.stack(k_p), jnp.stack(v_p), jnp.stack(k_s), jnp.stack(v_s), jnp.stack(gv_s))
```

```python
import os
from contextlib import ExitStack
import numpy as np
import concourse.bass as bass
import concourse.mybir as mybir
from concourse.bass_utils import run_bass_kernel_spmd

F32 = mybir.dt.float32
F32R = mybir.dt.float32r
AF = mybir.ActivationFunctionType
ALU = mybir.AluOpType

ENGS = ["pe", "act", "dve", "pool", "sp"]
L = 4
D = 1024
DFF = 2816
NCORE = 8
NPB = 18
NBLK = 19
A_LEN = NPB * 128
B_START = 4096 - A_LEN
EPS = 1e-6


class Tok:
    __slots__ = ("w", "r")

    def __init__(self):
        self.w = None
        self.r = []


class Op:
    __slots__ = ("eng", "fn", "deps", "is_dma", "key", "cnt", "need_inc", "n_instr")

    def __init__(self, eng, fn):
        self.eng = eng
        self.fn = fn
        self.deps = ()
        self.is_dma = False
        self.key = None
        self.cnt = 0
        self.need_inc = False
        self.n_instr = 1


class Prog:
    def __init__(self, nc):
        self.nc = nc
        self.ops = {e: [] for e in ENGS}
        self.all_ops = []
        self.dma_keys = {}
        self.stack = ExitStack()

    def sbuf(self, name, shape, dtype):
        return self.stack.enter_context(self.nc.sbuf_tensor("sb_" + name, shape, dtype))

    def psum(self, name, shape, dtype):
        return self.stack.enter_context(self.nc.psum_tensor("ps_" + name, shape, dtype))

    def op(self, eng, fn, reads=(), writes=()):
        o = Op(eng, fn)
        deps = set()
        for t in reads:
            if t.w is not None:
                deps.add(t.w)
        for t in writes:
            if t.w is not None:
                deps.add(t.w)
            deps.update(t.r)
        o.deps = deps
        for t in reads:
            t.r.append(o)
        for t in writes:
            t.w = o
            t.r = []
        self.ops[eng].append(o)
        self.all_ops.append(o)
        return o

    def dma(self, key, fn, reads=(), writes=(), n=1, eng="sp"):
        o = self.op(eng, fn, reads, writes)
        o.is_dma = True
        o.key = key
        o.n_instr = n
        ent = self.dma_keys.setdefault(key, [0, None])
        if ent[1] is not None:
            o.deps.add(ent[1])
        ent[0] += 16 * n
        ent[1] = o
        o.cnt = ent[0]
        return o

    def emit(self):
        nc = self.nc
        for o in self.all_ops:
            for d in o.deps:
                if not d.is_dma:
                    if d.eng == "pe" and o.eng == "pe" and not o.is_dma:
                        continue
                    d.need_inc = True
        for e in ENGS:
            c = 0
            for o in self.ops[e]:
                if o.is_dma:
                    continue
                if o.need_inc:
                    c += 1
                    o.cnt = c
        st = self.stack
        esem = {e: st.enter_context(nc.semaphore("s_" + e)) for e in ENGS}
        ksem = {k: st.enter_context(nc.semaphore("k_%d" % i)) for i, k in enumerate(self.dma_keys)}
        block = st.enter_context(nc.Block())

        def run(e, h):
            waited = {}
            for o in self.ops[e]:
                need = {}
                for d in o.deps:
                    if d.is_dma:
                        s = ksem[d.key]
                        sk = ("k", d.key)
                    else:
                        if d.eng == "pe" and e == "pe" and not o.is_dma:
                            continue
                        s = esem[d.eng]
                        sk = ("e", d.eng)
                    if d.cnt > waited.get(sk, 0) and d.cnt > need.get(sk, (None, 0))[1]:
                        need[sk] = (s, d.cnt)
                for sk, (s, v) in need.items():
                    h.wait_ge(s, v)
                    waited[sk] = v
                r = o.fn(h)
                if o.is_dma:
                    rs = r if isinstance(r, (list, tuple)) else [r]
                    assert len(rs) == o.n_instr
                    for i in rs:
                        i.then_inc(ksem[o.key], 16)
                elif o.need_inc:
                    r.then_inc(esem[e], 1)
            if e == "sp":
                for k, ent in self.dma_keys.items():
                    if ent[0] > waited.get(("k", k), 0):
                        h.wait_ge(ksem[k], ent[0])

        @block.tensor
        def _(h):
            run("pe", h)

        @block.scalar
        def _(h):
            run("act", h)

        @block.vector
        def _(h):
            run("dve", h)

        @block.gpsimd
        def _(h):
            run("pool", h)

        @block.sync
        def _(h):
            run("sp", h)

    def close(self):
        self.stack.close()


WSLOT = 4096
NWS = 3


def layer_schedule():
    s = []
    for f in (1, 2):
        for hf in range(2):
            for j in range(11):
                s.append(("gu", f, hf * 11 + j, 2048))
            for dp in range(4):
                s.append(("wd", f, hf, dp, 2816))
        if f == 1:
            for c in range(4):
                s.append(("glu", c, 2048))
            s.append(("u", 4096))
            s.append(("vg", 4096))
            s.append(("q", 4096))
            s.append(("kv", 2048))
            for j in range(8):
                s.append(("oproj", j, 1536))
                s.append(("gate", j, 3072))
            for dp in range(4):
                s.append(("wout", dp, 2048))
    return s


SCHED = layer_schedule()
WCOLS = sum(e[-1] for e in SCHED)

PC_N1, PC_NM, PC_N2 = 0, 8, 16
PC_WDW = 24
PC_BC = 148
PC_LG = 152
PC_LB = 156
PC_QG = 160
PC_KG = 161
PC_SK = 162
NPAR = 166

CC_ID = 0
CC_ONES = 128
CC_BD = 256
CC_OLO = 384
CC_OHI = 512
CC_PROT = 640
CC_MCUR = 768
CC_MPREV = 896
CC_MNEW = 1024
CC_MPAST = 1152
NCONST = 1160


def q_perm():
    idx = []
    for g in range(4):
        idx += list(range(g * 64, g * 64 + 64)) + list(range((4 + g) * 64, (4 + g) * 64 + 64))
    return np.array(idx)


def pack_weights(inp):
    W = np.empty((L, 128, WCOLS), np.float32)
    qp = q_perm()
    for l in range(L):
        off = 0
        win = inp["w_in"][l]
        for e in SCHED:
            n = e[-1]
            if e[0] == "gu":
                w = (inp["w_ffn1_gu"] if e[1] == 1 else inp["w_ffn2_gu"])[l]
                j = e[2]
                g = w[:, j * 128:(j + 1) * 128].reshape(8, 128, 128)
                u = w[:, DFF + j * 128:DFF + (j + 1) * 128].reshape(8, 128, 128)
                t = np.concatenate([g, u], axis=2).transpose(1, 0, 2)
            elif e[0] == "wd":
                w = (inp["w_ffn1_down"] if e[1] == 1 else inp["w_ffn2_down"])[l]
                hf, dp = e[2], e[3]
                t = w[hf * 11 * 128:(hf + 1) * 11 * 128, dp * 256:(dp + 1) * 256].reshape(11, 128, 256).transpose(1, 0, 2)
            elif e[0] == "glu":
                c = e[1]
                a = win[:, c * 128:(c + 1) * 128].reshape(8, 128, 128)
                g = win[:, 512 + c * 128:512 + (c + 1) * 128].reshape(8, 128, 128)
                t = np.concatenate([a, g], axis=2).transpose(1, 0, 2)
            elif e[0] == "u":
                t = win[:, 1024:1536].reshape(8, 128, 512).transpose(1, 0, 2)
            elif e[0] == "vg":
                t = win[:, 1536:2048].reshape(8, 128, 512).transpose(1, 0, 2)
            elif e[0] == "q":
                t = win[:, 2048:2560][:, qp].reshape(8, 128, 512).transpose(1, 0, 2)
            elif e[0] == "kv":
                t = win[:, 2560:2816].reshape(8, 128, 256).transpose(1, 0, 2)
            elif e[0] == "gate":
                j = e[1]
                gs = [win[:, 2816 + br * 1024 + j * 128:2816 + br * 1024 + (j + 1) * 128].reshape(8, 128, 1, 128)
                      for br in range(3)]
                t = np.concatenate(gs, axis=2).transpose(1, 0, 2, 3)
            elif e[0] == "oproj":
                j = e[1]
                wo = inp["w_o"][l][qp]
                ps = [m[:, j * 128:(j + 1) * 128].reshape(1, 4, 128, 128)
                      for m in (inp["w_conv_out"][l], inp["w_gmlp_out"][l], wo)]
                t = np.concatenate(ps, axis=0).transpose(2, 0, 1, 3)
            elif e[0] == "wout":
                dp = e[1]
                t = inp["w_out"][l][:, dp * 256:(dp + 1) * 256].reshape(8, 128, 256).transpose(1, 0, 2)
            W[l, :, off:off + n] = t.reshape(128, n)
            off += n
    return W


def pack_params(inp):
    Pm = np.zeros((128, L, NPAR), np.float32)
    for l in range(L):
        for nm, c0 in (("norm_ffn1", PC_N1), ("norm_mix", PC_NM), ("norm_ffn2", PC_N2)):
            Pm[:, l, c0:c0 + 8] = inp[nm][l].reshape(8, 128).T
        Pm[:, l, PC_WDW:PC_WDW + 124] = inp["w_conv_dw"][l].reshape(31, 4, 128).transpose(2, 1, 0).reshape(128, 124)
        Pm[:, l, PC_BC:PC_BC + 4] = inp["b_conv_dw"][l].reshape(4, 128).T
        Pm[:, l, PC_LG:PC_LG + 4] = inp["ln_conv_g"][l].reshape(4, 128).T
        Pm[:, l, PC_LB:PC_LB + 4] = inp["ln_conv_b"][l].reshape(4, 128).T
        Pm[:, l, PC_QG] = np.tile(inp["q_norm_g"][l], 2)
        Pm[:, l, PC_KG] = np.tile(inp["k_norm_g"][l], 2)
        sk = inp["attn_sinks"][l]
        Pm[:64, l, PC_SK:PC_SK + 4] = sk[None, 0:4]
        Pm[64:, l, PC_SK:PC_SK + 4] = sk[None, 4:8]
    return Pm


def make_consts():
    C = np.zeros((128, NCONST), np.float32)
    C[:, CC_ID:CC_ID + 128] = np.eye(128)
    C[:, CC_ONES:CC_ONES + 128] = 1.0
    C[:64, CC_BD:CC_BD + 64] = 1.0
    C[64:, CC_BD + 64:CC_BD + 128] = 1.0
    C[:, CC_OLO:CC_OLO + 64] = 1.0
    C[:, CC_OHI + 64:CC_OHI + 128] = 1.0
    for m in range(128):
        if m % 64 < 32:
            C[m + 32, CC_PROT + m] = -1.0
        else:
            C[m - 32, CC_PROT + m] = 1.0
    s = np.arange(128)[:, None]
    q = np.arange(128)[None, :]
    C[:, CC_MCUR:CC_MCUR + 128] = (s <= q)
    C[:, CC_MPREV:CC_MPREV + 128] = (s >= q)
    C[:, CC_MNEW:CC_MNEW + 128] = ((s // 8) == (q // 8)) & ((s % 8) <= (q % 8))
    C[:, CC_MPAST:CC_MPAST + 8] = (np.arange(128)[:, None] >= np.arange(8)[None, :])
    return C


def make_gsel():
    G = np.zeros((128, 4, 128), np.float32)
    for cp in range(4):
        G[2 * cp, cp, :64] = 1.0
        G[2 * cp + 1, cp, 64:] = 1.0
    return G


def rope_tables(start):
    half = 32
    inv_freq = (1.0 / (10000.0 ** (np.arange(half, dtype=np.float32) / half))).astype(np.float32)
    pos = np.concatenate([start + np.arange(A_LEN), 8192 + (np.arange(128) % 8)]).astype(np.float32)
    ang = pos[None, :] * inv_freq[np.arange(128) % 32][:, None]
    return np.cos(ang).astype(np.float32), np.sin(ang).astype(np.float32)


def build(depth=L, tiles=None):
    if tiles is None:
        tiles = [(0, 4), (4, 4), (8, 4), (12, 4), (16, 3)]
    nc = bass.Bass("TRN2", target_bir_lowering=False)
    dt_in = lambda n, s: nc.dram_tensor(n, s, F32, kind="ExternalInput").ap()
    dt_out = lambda n, s: nc.dram_tensor(n, s, F32, kind="ExternalOutput").ap()
    xin_d = dt_in("xin", [NBLK * 128, D])
    wl_d = dt_in("wl", [L, 128, WCOLS])
    par_d = dt_in("par", [128, L, NPAR])
    con_d = dt_in("con", [128, NCONST])
    gsel_d = dt_in("gsel", [128, 4, 128])
    cos_d = dt_in("cos", [128, NBLK * 128])
    sin_d = dt_in("sin", [128, NBLK * 128])
    lnp_d = dt_in("lnp", [L, 4, 512])
    wsT_d = dt_in("wsT", [L, 128, 8, 128])
    wsS_d = dt_in("wsS", [L, 128, 8, 128])
    bsp_d = dt_in("bsp", [L, 128, 256])
    stc_d = dt_in("stc", [L, 16, 30, 512])
    ck_d = dt_in("ck", [L, 16, 128, 128])
    cv_d = dt_in("cv", [L, 16, 128, 128])
    y_d = dt_out("y", [NBLK * 128, D])
    convp_d = dt_out("convp", [L, 30, 512])
    convs_d = dt_out("convs", [L, 16, 30, 512])
    kp_d = dt_out("kp", [L, 128, 128])
    vp_d = dt_out("vp", [L, 128, 128])
    ks_d = dt_out("ks", [L, 128, 128])
    vs_d = dt_out("vs", [L, 128, 128])
    gv_d = dt_out("gv", [L, 128, 512])
    DEBUG = bool(int(os.environ.get("MK_DEBUG", "0")))
    if DEBUG:
        dbg_d = dt_out("dbgo", [6, 128, 2048])

    P = Prog(nc)
    SKIP = os.environ.get("MK_SKIP", "").split(",")
    _dma = P.dma

    def dma_f(key, fn, reads=(), writes=(), n=1, eng="sp"):
        if key in SKIP:
            return None
        return _dma(key, fn, reads=reads, writes=writes, n=n, eng=eng)
    P.dma = dma_f
    WS = [P.sbuf("ws%d" % i, [128, WSLOT], F32R) for i in range(NWS)]
    WS_t = [Tok() for _ in range(NWS)]
    xT = P.sbuf("xT", [128, 8, 512], F32)
    xT_t = [Tok() for _ in range(8)]
    hT = P.sbuf("hT", [128, 8, 512], F32R)
    hT_t = [Tok() for _ in range(8)]
    par = P.sbuf("par", [128, L, NPAR], F32)
    par_t = Tok()
    con = P.sbuf("con", [128, NCONST], F32)
    conR = P.sbuf("conR", [128, NCONST], F32R)
    con_t = Tok()
    gsel = P.sbuf("gsel", [128, 4, 128], F32R)
    sinkexp = P.sbuf("sinkexp", [128, L, 4], F32)
    sk_t = Tok()
    Atail = P.sbuf("Atail", [128, L, 4, 30], F32R)
    Khalo = P.sbuf("Khalo", [128, L, 128], F32R)
    VZhalo = P.sbuf("VZhalo", [128, L, 2, 128], F32R)
    halo_t = [[Tok() for _ in range(3)] for _ in range(L)]
    stat = P.sbuf("stat", [128, 2, 12], F32)
    stat_t = [Tok(), Tok()]
    NPR, NPF = 36, 15
    SCR = P.sbuf("scr", [128, NPR * 512], F32R)
    SCF = P.sbuf("scf", [128, NPF * 512], F32)
    PGR_t = [Tok() for _ in range(NPR)]
    PGF_t = [Tok() for _ in range(NPF)]
    BANK = [P.psum("bank%d" % i, [128, 512], F32) for i in range(8)]
    BANK_t = [Tok() for _ in range(8)]
    bank_free = list(range(8))

    def balloc():
        return bank_free.pop(0)

    def bfree(b):
        bank_free.append(b)

    class Buf:
        def __init__(self, pg0, ncols, sp="R"):
            self.c0 = pg0 * 512
            self.n = ncols
            self.sp = sp

        def r(self, lo=0, hi=None):
            assert self.sp == "R"
            hi = self.n if hi is None else hi
            return SCR[:, self.c0 + lo:self.c0 + hi]

        def f(self, lo=0, hi=None):
            hi = self.n if hi is None else hi
            if self.sp == "R":
                return SCR[:, self.c0 + lo:self.c0 + hi].bitcast(F32)
            return SCF[:, self.c0 + lo:self.c0 + hi]

        def tk(self, lo=0, hi=None):
            hi = self.n if hi is None else hi
            a = (self.c0 + lo) // 512
            b = (self.c0 + hi - 1) // 512
            return (PGR_t if self.sp == "R" else PGF_t)[a:b + 1]

    TMPR = [Buf(0, 512), Buf(1, 512)]
    TMPF = [Buf(0, 512, "F"), Buf(1, 512, "F")]
    R0 = Buf(2, 512, "F")
    R1 = Buf(3, 512, "F")
    R2 = Buf(4, 512, "F")
    tmp_i = [0, 0]

    def ntmpr():
        tmp_i[0] ^= 1
        return TMPR[tmp_i[0]]

    def ntmpf():
        tmp_i[1] ^= 1
        return TMPF[tmp_i[1]]

    ident = con[:, CC_ID:CC_ID + 128]
    onesR = conR[:, CC_ONES:CC_ONES + 128]
    bdR = conR[:, CC_BD:CC_BD + 128]
    oloR = conR[:, CC_OLO:CC_OLO + 128]
    ohiR = conR[:, CC_OHI:CC_OHI + 128]
    protR = conR[:, CC_PROT:CC_PROT + 128]
    mcur = con[:, CC_MCUR:CC_MCUR + 128]
    mprev = con[:, CC_MPREV:CC_MPREV + 128]
    mnew = con[:, CC_MNEW:CC_MNEW + 128]
    mpast = con[:, CC_MPAST:CC_MPAST + 8]

    zcol = con[:, CC_OLO + 64:CC_OLO + 65]
    P.dma("par", lambda e: e.dma_start(out=par[:], in_=par_d), writes=[par_t])
    P.dma("con", lambda e: e.dma_start(out=con[:], in_=con_d), writes=[con_t])
    P.dma("conR", lambda e: [e.dma_start(out=conR[:], in_=con_d), e.dma_start(out=gsel[:], in_=gsel_d)],
          writes=[con_t], n=2, eng="pool")
    P.op("act", lambda e: e.activation(sinkexp[:], par[:, :, PC_SK:PC_SK + 4], AF.Exp), reads=[par_t], writes=[sk_t])
    for l in range(L):
        P.op("pool", (lambda l: lambda e: e.tensor_copy(Atail[:, l].rearrange("p a b -> p (a b)"), zcol.broadcast_to([128, 120])))(l),
             reads=[con_t], writes=[halo_t[l][0]])

    wlist = []
    for (b0, nb) in tiles:
        for l in range(depth):
            off = 0
            for e_ in SCHED:
                wlist.append((l, off, e_[-1]))
                off += e_[-1]
    wstate = {"issued": 0, "next": 0}

    def w_issue(upto):
        while wstate["issued"] < min(upto, len(wlist)):
            i = wstate["issued"]
            l, off, n = wlist[i]
            s = i % NWS
            P.dma("ws%d" % s, (lambda s, l, off, n: lambda e: e.dma_start(out=WS[s][:, 0:n], in_=wl_d[l, :, off:off + n]))(s, l, off, n),
                  writes=[WS_t[s]], eng="pool")
            wstate["issued"] += 1

    def wnext(n_expect):
        i = wstate["next"]
        assert wlist[i][2] == n_expect, (wlist[i], n_expect)
        w_issue(i + NWS)
        wstate["next"] += 1
        s = i % NWS
        return WS[s], WS_t[s]

    def par_c(l, c):
        return par[:, l, c:c + 1]

    def rmsnorm(l, pc, T):
        bs = balloc()
        for k in range(8):
            t = ntmpr()
            P.op("act", (lambda t, k: lambda e: e.activation(t.r(0, T), xT[:, k, 0:T], AF.Square))(t, k),
                 reads=[xT_t[k]], writes=t.tk())
            P.op("pe", (lambda t, k: lambda e: e.matmul(BANK[bs][:, 0:T], onesR, t.r(0, T), start=(k == 0), stop=(k == 7)))(t, k),
                 reads=t.tk() + [con_t], writes=[BANK_t[bs]])
        P.op("act", lambda e: e.activation(R0.f(0, T), BANK[bs][:, 0:T], AF.Ln, bias=EPS, scale=1.0 / D),
             reads=[BANK_t[bs]], writes=R0.tk())
        bfree(bs)
        P.op("act", lambda e: e.activation(R0.f(0, T), R0.f(0, T), AF.Exp, scale=-0.5), reads=R0.tk(), writes=R0.tk())
        for k in range(8):
            P.op("dve", (lambda k: lambda e: e.scalar_tensor_tensor(out=hT[:, k, 0:T], in0=xT[:, k, 0:T], scalar=par_c(l, pc + k),
                                                                      in1=R0.f(0, T), op0=ALU.mult, op1=ALU.mult))(k),
                 reads=[xT_t[k], par_t] + R0.tk(), writes=[hT_t[k]])

    ACT_B = Buf(2, 11 * 512)

    def ffn(l, T):
        for hf in range(2):
            for j in range(11):
                w, wt = wnext(2048)
                wv = w[:, 0:2048].rearrange("p (k c) -> p k c", k=8)
                bg, bu = balloc(), balloc()
                for (bb, c0) in ((bg, 0), (bu, 128)):
                    for k in range(8):
                        P.op("pe", (lambda bb, c0, k, wv: lambda e: e.matmul(BANK[bb][:, 0:T], wv[:, k, c0:c0 + 128], hT[:, k, 0:T],
                                                                            start=(k == 0), stop=(k == 7)))(bb, c0, k, wv),
                             reads=[wt, hT_t[k]], writes=[BANK_t[bb]])
                t = ntmpf()
                P.op("act", (lambda t, bg: lambda e: e.activation(t.f(0, T), BANK[bg][:, 0:T], AF.Silu))(t, bg),
                     reads=[BANK_t[bg]], writes=t.tk())
                P.op("dve", (lambda t, bu, j: lambda e: e.tensor_tensor(out=ACT_B.r(j * 512, j * 512 + T), in0=t.f(0, T), in1=BANK[bu][:, 0:T],
                                                                         op=ALU.mult))(t, bu, j),
                     reads=t.tk() + [BANK_t[bu]], writes=ACT_B.tk(j * 512, j * 512 + T))
                bfree(bg)
                bfree(bu)
            for dp in range(4):
                w, wt = wnext(2816)
                wv = w[:, 0:2816].rearrange("p (k c) -> p k c", k=11)
                for d in range(2):
                    by = balloc()
                    for fi in range(11):
                        P.op("pe", (lambda by, fi, d, wv: lambda e: e.matmul(BANK[by][:, 0:T], wv[:, fi, d * 128:(d + 1) * 128],
                                                                            ACT_B.r(fi * 512, fi * 512 + T), start=(fi == 0), stop=(fi == 10)))(by, fi, d, wv),
                             reads=[wt] + ACT_B.tk(fi * 512, fi * 512 + T), writes=[BANK_t[by]])
                    kk = dp * 2 + d
                    P.op("dve", (lambda by, kk: lambda e: e.scalar_tensor_tensor(out=xT[:, kk, 0:T], in0=BANK[by][:, 0:T], scalar=0.5,
                                                                                  in1=xT[:, kk, 0:T], op0=ALU.mult, op1=ALU.add))(by, kk),
                         reads=[BANK_t[by], xT_t[kk]], writes=[xT_t[kk]])
                    bfree(by)

    AST = 896
    AT = Buf(2, 4 * AST)
    DGC = [Buf(13, 31 * 128), Buf(21, 31 * 128)]
    CB = Buf(9, 4 * 512)
    VGE = [Buf(2, 512), Buf(3, 512), Buf(6, 512), Buf(7, 512)]
    VGO = [Buf(4, 512), Buf(5, 512), Buf(8, 512), Buf(18, 512)]
    WSM = Buf(13, 1024)
    WSSM = Buf(15, 1024)
    BSP = Buf(17, 256)
    QT = Buf(2, 4 * 512)
    QN = [Buf(6, 512), Buf(11, 512, "F")]
    KT = Buf(7, 1024)
    VZ = Buf(9, 5 * 256)
    PT = [Buf(12, 512), Buf(13, 512), Buf(14, 512), Buf(15, 512)]
    KCT = Buf(16, 4 * 512)
    VZC = Buf(20, 4 * 512)
    ON = Buf(24, 4 * 512)
    UB = Buf(28, 4 * 512)
    SA = Buf(32, 4 * 512)
    MIX = Buf(2, 8 * 512)
    STG = [Buf(5, 1024, "F"), Buf(7, 1024, "F")]
    STC = Buf(5, 4 * 512, "F")
    LNP = Buf(5, 4 * 512, "F")
    KCR = Buf(5, 4 * 512, "F")
    VT = [Buf(9, 512, "F"), Buf(10, 512, "F"), Buf(13, 512, "F"), Buf(14, 512, "F")]
    COS = Buf(9, 512, "F")
    SIN = Buf(10, 512, "F")
    WSR = Buf(11, 1024, "F")
    WSSR = Buf(13, 1024, "F")

    STOP = os.environ.get("MK_STOP", "")

    class StopBuild(Exception):
        pass

    def stopchk(stage):
        if STOP and STOP == stage:
            raise StopBuild()

    def mixer(l, ti, b0, nb, T, Tp, has_s):
        tile0 = (ti == 0)

        def build_diag(c, j0, j1):
            dg = DGC[c % 2]
            for j in range(j0, j1):
                P.op("dve", (lambda dg, j, c: lambda e: e.tensor_scalar(out=dg.r(j * 128, (j + 1) * 128), in0=ident,
                                                                          scalar1=par_c(l, PC_WDW + c * 31 + j), scalar2=None, op0=ALU.mult))(dg, j, c),
                     reads=[con_t, par_t], writes=dg.tk(j * 128, (j + 1) * 128))

        P.op("pool", lambda e: e.tensor_copy(AT.r().rearrange("p (c x) -> p c x", c=4)[:, :, 0:30], Atail[:, l]),
             reads=[halo_t[l][0]], writes=AT.tk())
        if has_s:
            stg = STC
            P.dma("stc", lambda e: e.dma_start(out=stg.f().rearrange("p (g c) -> p g c", g=4)[0:120],
                                               in_=stc_d[l].rearrange("(g s) r c -> (s r) g c", g=4)), writes=stg.tk())
            P.dma("convs_cp", lambda e: [e.dma_start(out=convs_d[l, g * 4 + s_, 0:22, :],
                                                     in_=stg.f().rearrange("p (g c) -> p g c", g=4)[s_ * 30 + 8:s_ * 30 + 30, g, :])
                                         for g in range(4) for s_ in range(4)], reads=stg.tk(), n=16)
            for c in range(4):
                bt = balloc()
                for g in range(4):
                    P.op("pe", (lambda bt, g, c: lambda e: e.transpose(BANK[bt][:, g * 120:(g + 1) * 120],
                                                                         stg.f().rearrange("p (g c) -> p g c", g=4)[0:120, g, c * 128:(c + 1) * 128],
                                                                         ident[0:120, 0:120]))(bt, g, c),
                         reads=stg.tk() + [con_t], writes=[BANK_t[bt]])
                P.op("act", (lambda bt, c: lambda e: e.copy(
                    AT.r(c * AST + 30 + Tp, c * AST + 30 + Tp + 608).rearrange("p (s x) -> p s x", s=16)[:, :, 0:30],
                    BANK[bt][:, 0:480].rearrange("p (s x) -> p s x", s=16)))(bt, c),
                    reads=[BANK_t[bt]], writes=AT.tk(c * AST, (c + 1) * AST))
                bfree(bt)
        stopchk("s1_stc_done")
        for c in range(4):
            build_diag(c // 2, (c % 2) * 16, min(31, (c % 2) * 16 + 16))
            w, wt = wnext(2048)
            wv = w[:, 0:2048].rearrange("p (k c) -> p k c", k=8)
            ba, bg = balloc(), balloc()
            for (bb, c0) in ((ba, 0), (bg, 128)):
                for k in range(8):
                    P.op("pe", (lambda bb, c0, k, wv: lambda e: e.matmul(BANK[bb][:, 0:T], wv[:, k, c0:c0 + 128], hT[:, k, 0:T],
                                                                        start=(k == 0), stop=(k == 7)))(bb, c0, k, wv),
                         reads=[wt, hT_t[k]], writes=[BANK_t[bb]])
            t = ntmpf()
            P.op("act", (lambda t, bg: lambda e: e.activation(t.f(0, T), BANK[bg][:, 0:T], AF.Sigmoid))(t, bg),
                 reads=[BANK_t[bg]], writes=t.tk())
            P.op("dve", (lambda t, ba, c: lambda e: e.tensor_tensor(out=AT.r(c * AST + 30, c * AST + 30 + Tp), in0=t.f(0, Tp),
                                                                     in1=BANK[ba][:, 0:Tp], op=ALU.mult))(t, ba, c),
                 reads=t.tk() + [BANK_t[ba]], writes=AT.tk(c * AST, (c + 1) * AST))
            if has_s:
                P.op("dve", (lambda t, ba, c: lambda e: e.tensor_tensor(
                    out=AT.r(c * AST + 30 + Tp, c * AST + 30 + Tp + 608).rearrange("p (s x) -> p s x", s=16)[:, :, 30:38],
                    in0=t.f(Tp, T).rearrange("p (s x) -> p s x", s=16),
                    in1=BANK[ba][:, Tp:T].rearrange("p (s x) -> p s x", s=16), op=ALU.mult))(t, ba, c),
                    reads=t.tk() + [BANK_t[ba]], writes=AT.tk(c * AST, (c + 1) * AST))
            bfree(ba)
            bfree(bg)
        stopchk("s2_glu_done")
        if not has_s:
            P.op("pool", lambda e: e.tensor_copy(Atail[:, l], AT.r().rearrange("p (c x) -> p c x", c=4)[:, :, Tp:Tp + 30]),
                 reads=AT.tk(), writes=[halo_t[l][0]])
        else:
            bt = balloc()
            for c in range(4):
                P.op("pe", (lambda c, bt: lambda e: e.transpose(BANK[bt][:, c * 128:(c + 1) * 128],
                                                             AT.f(c * AST + Tp - 98, c * AST + Tp + 30), ident))(c, bt),
                     reads=AT.tk(c * AST, (c + 1) * AST) + [con_t], writes=[BANK_t[bt]])
            st0 = ntmpf()
            P.op("act", (lambda bt: lambda e: e.copy(st0.f(), BANK[bt][:, :]))(bt), reads=[BANK_t[bt]], writes=st0.tk())
            bfree(bt)
            P.dma("convp", lambda e: e.dma_start(out=convp_d[l], in_=st0.f()[98:128]), reads=st0.tk())
            bt = balloc()
            for c in range(4):
                tc_ = ntmpf()
                P.op("dve", (lambda c, tc_: lambda e: e.tensor_copy(
                    tc_.f(0, 128).rearrange("p (s x) -> p s x", s=16),
                    AT.f(c * AST + 30 + Tp, c * AST + 30 + Tp + 608).rearrange("p (s x) -> p s x", s=16)[:, :, 30:38]))(c, tc_),
                    reads=AT.tk(c * AST, (c + 1) * AST), writes=tc_.tk())
                P.op("pe", (lambda c, tc_, bt: lambda e: e.transpose(BANK[bt][:, c * 128:(c + 1) * 128], tc_.f(0, 128), ident))(c, tc_, bt),
                     reads=tc_.tk() + [con_t], writes=[BANK_t[bt]])
            st1 = ntmpf()
            P.op("act", (lambda bt: lambda e: e.copy(st1.f(), BANK[bt][:, :]))(bt), reads=[BANK_t[bt]], writes=st1.tk())
            bfree(bt)
            P.dma("convs_new", lambda e: [e.dma_start(out=convs_d[l, s, 22:30, :], in_=st1.f()[s * 8:(s + 1) * 8]) for s in range(16)],
                  reads=st1.tk(), n=16)
        stopchk("s3_convout_done")
        bs1, bs2 = balloc(), balloc()
        def conv_chunk(c):
            bc = balloc()
            dg = DGC[c % 2]
            for j in range(31):
                P.op("pe", (lambda dg, j, c, bc: lambda e: e.matmul(BANK[bc][:, 0:Tp], dg.r(j * 128, (j + 1) * 128),
                                                                   AT.r(c * AST + j, c * AST + j + Tp), start=(j == 0), stop=(j == 30)))(dg, j, c, bc),
                     reads=dg.tk(j * 128, (j + 1) * 128) + AT.tk(c * AST, (c + 1) * AST), writes=[BANK_t[bc]])
            if has_s:
                bsa, bsb = balloc(), balloc()
                a0 = c * AST + 30 + Tp
                for j in range(31):
                    P.op("pe", (lambda dg, j, a0, bsa: lambda e: e.matmul(BANK[bsa][:, 0:464], dg.r(j * 128, (j + 1) * 128),
                                                                       AT.r(a0 + j, a0 + j + 464), start=(j == 0), stop=(j == 30)))(dg, j, a0, bsa),
                         reads=dg.tk(j * 128, (j + 1) * 128) + AT.tk(c * AST, (c + 1) * AST), writes=[BANK_t[bsa]])
                    P.op("pe", (lambda dg, j, a0, bsb: lambda e: e.matmul(BANK[bsb][:, 0:84], dg.r(j * 128, (j + 1) * 128),
                                                                       AT.r(a0 + 494 + j, a0 + 494 + j + 84), start=(j == 0), stop=(j == 30)))(dg, j, a0, bsb),
                         reads=dg.tk(j * 128, (j + 1) * 128) + AT.tk(c * AST, (c + 1) * AST), writes=[BANK_t[bsb]])
            if c + 2 < 4:
                build_diag(c + 2, 0, 31)
            t = ntmpr()
            segs = [(BANK[bc][:, 0:Tp], 0, Tp, bc)]
            if has_s:
                segs.append((BANK[bsa][:, 0:494].rearrange("p (s x) -> p s x", x=38)[:, :, 0:8], Tp, Tp + 104, bsa))
                segs.append((BANK[bsb][:, 0:114].rearrange("p (s x) -> p s x", x=38)[:, :, 0:8], Tp + 104, T, bsb))
            for (src, lo, hi, bsrc) in segs:
                def shp(ap, src=src):
                    return ap.rearrange("p (s x) -> p s x", x=8) if len(src.shape) == 3 else ap
                P.op("act", (lambda c, src, lo, hi, shp: lambda e: e.activation(shp(CB.r(c * 512 + lo, c * 512 + hi)), src, AF.Identity,
                                                                              bias=par_c(l, PC_BC + c)))(c, src, lo, hi, shp),
                     reads=[BANK_t[bsrc], par_t], writes=CB.tk(c * 512, c * 512 + T))
                P.op("act", (lambda c, src, lo, hi, shp, t: lambda e: e.activation(shp(t.r(lo, hi)), src, AF.Square, bias=par_c(l, PC_BC + c)))(c, src, lo, hi, shp, t),
                     reads=[BANK_t[bsrc], par_t], writes=t.tk())
            bfree(bc)
            if has_s:
                bfree(bsa)
                bfree(bsb)
            P.op("pe", (lambda c: lambda e: e.matmul(BANK[bs1][:, 0:T], onesR, CB.r(c * 512, c * 512 + T), start=(c == 0), stop=(c == 3)))(c),
                 reads=CB.tk(c * 512, c * 512 + T) + [con_t], writes=[BANK_t[bs1]])
            P.op("pe", (lambda c, t: lambda e: e.matmul(BANK[bs2][:, 0:T], onesR, t.r(0, T), start=(c == 0), stop=(c == 3)))(c, t),
                 reads=t.tk() + [con_t], writes=[BANK_t[bs2]])
        for c in range(4):
            conv_chunk(c)
        w, wt = wnext(4096)
        wv = w[:, 0:4096].rearrange("p (k c) -> p k c", k=8)
        for cu in range(4):
            bu = balloc()
            for k in range(8):
                P.op("pe", (lambda bu, k, cu, wv: lambda e: e.matmul(BANK[bu][:, 0:T], wv[:, k, cu * 128:(cu + 1) * 128], hT[:, k, 0:T],
                                                                    start=(k == 0), stop=(k == 7)))(bu, k, cu, wv),
                     reads=[wt, hT_t[k]], writes=[BANK_t[bu]])
            P.op("act", (lambda bu, cu: lambda e: e.activation(UB.r(cu * 512, cu * 512 + T), BANK[bu][:, 0:T], AF.Gelu_apprx_tanh))(bu, cu),
                 reads=[BANK_t[bu]], writes=UB.tk(cu * 512, cu * 512 + T))
            bfree(bu)
        def ln_a():
            t0 = ntmpf()
            P.op("act", lambda e: e.mul(R1.f(0, T), BANK[bs1][:, 0:T], 1.0 / 512), reads=[BANK_t[bs1]], writes=R1.tk())
            P.op("dve", lambda e: e.tensor_tensor(out=t0.f(0, T), in0=R1.f(0, T), in1=R1.f(0, T), op=ALU.mult), reads=R1.tk(), writes=t0.tk())
            P.op("dve", lambda e: e.scalar_tensor_tensor(out=R0.f(0, T), in0=BANK[bs2][:, 0:T], scalar=1.0 / 512, in1=t0.f(0, T),
                                                         op0=ALU.mult, op1=ALU.subtract), reads=[BANK_t[bs2]] + t0.tk(), writes=R0.tk())
            bfree(bs1)
            bfree(bs2)
            P.op("act", lambda e: e.activation(R0.f(0, T), R0.f(0, T), AF.Ln, bias=EPS), reads=R0.tk(), writes=R0.tk())
            P.op("act", lambda e: e.activation(R0.f(0, T), R0.f(0, T), AF.Exp, scale=-0.5), reads=R0.tk(), writes=R0.tk())
            t1 = R2
            P.op("dve", lambda e: e.tensor_tensor(out=t1.f(0, T), in0=R1.f(0, T), in1=R0.f(0, T), op=ALU.mult), reads=R0.tk() + R1.tk(), writes=t1.tk())

        def ln_b(c0_, c1_):
            t1 = R2
            for c in range(c0_, c1_):
                tq = ntmpf()
                P.op("dve", (lambda c, tq: lambda e: e.tensor_tensor(out=tq.f(0, T), in0=CB.f(c * 512, c * 512 + T), in1=R0.f(0, T),
                                                                  op=ALU.mult))(c, tq), reads=CB.tk(c * 512, c * 512 + T) + R0.tk(), writes=tq.tk())
                P.op("dve", (lambda c, tq: lambda e: e.tensor_tensor(out=tq.f(0, T), in0=tq.f(0, T), in1=t1.f(0, T),
                                                                  op=ALU.subtract))(c, tq), reads=tq.tk() + t1.tk(), writes=tq.tk())
                P.op("act", (lambda c, tq: lambda e: e.activation(SA.r(c * 512, c * 512 + T), tq.f(0, T), AF.Silu,
                                                               bias=par_c(l, PC_LB + c), scale=par_c(l, PC_LG + c)))(c, tq),
                     reads=tq.tk() + [par_t], writes=SA.tk(c * 512, c * 512 + T))

        stopchk("s4_conv_done")
        P.dma("lnp", lambda e: e.dma_start(out=LNP.f().rearrange("p (a c) -> p a c", a=4), in_=lnp_d[l:l + 1].broadcast_to([128, 4, 512])),
              writes=LNP.tk())
        P.dma("wsr", lambda e: e.dma_start(out=WSR.f().rearrange("p (g t) -> p g t", g=8), in_=wsT_d[l]), writes=WSR.tk())
        P.op("pool", lambda e: e.tensor_tensor(out=WSM.r().rearrange("p (g t) -> p g t", g=8), in0=WSR.f().rearrange("p (g t) -> p g t", g=8),
                                               in1=mcur.unsqueeze(1).broadcast_to([128, 8, 128]), op=ALU.mult),
             reads=WSR.tk() + [con_t], writes=WSM.tk())
        P.dma("bsp", lambda e: e.dma_start(out=BSP.r(), in_=bsp_d[l]), writes=BSP.tk(), eng="pool")
        if has_s:
            P.dma("wssr", lambda e: e.dma_start(out=WSSR.f().rearrange("p (g t) -> p g t", g=8), in_=wsS_d[l]), writes=WSSR.tk())
            P.op("pool", lambda e: e.tensor_tensor(out=WSSM.r().rearrange("p (g t) -> p g t", g=8), in0=WSSR.f().rearrange("p (g t) -> p g t", g=8),
                                                   in1=mcur.unsqueeze(1).broadcast_to([128, 8, 128]), op=ALU.mult),
                 reads=WSSR.tk() + [con_t], writes=WSSM.tk())
        stopchk("g0")
        w, wt = wnext(4096)
        wv = w[:, 0:4096].rearrange("p (k c) -> p k c", k=8)
        stopchk("g1")
        nvt = 2 if has_s else 4
        bsg = [balloc() for _ in range(4)]
        lnv = LNP.f().rearrange("p (a c) -> p a c", a=4)
        def v_part(b):
            is_s = has_s and b == nb - 1
            bv = balloc()
            for k in range(8):
                P.op("pe", (lambda bv, k, b, wv: lambda e: e.matmul(BANK[bv][:, :], hT[:, k, b * 128:(b + 1) * 128], wv[:, k, :],
                                                                   start=(k == 0), stop=(k == 7)))(bv, k, b, wv),
                     reads=[wt, hT_t[k]], writes=[BANK_t[bv]])
            vt, vge, vgo = VT[b % nvt], VGE[b], VGO[b]
            sti = b % 2
            sv = stat[:, sti]
            P.op("act", (lambda bv, vt: lambda e: e.activation(vt.f(), BANK[bv][:, :], AF.Gelu_apprx_tanh))(bv, vt),
                 reads=[BANK_t[bv]], writes=vt.tk())
            bfree(bv)
            P.op("dve", (lambda vt, sv: lambda e: e.bn_stats(sv[:, 0:6], vt.f()))(vt, sv), reads=vt.tk(), writes=[stat_t[sti]])
            P.op("dve", (lambda sv: lambda e: e.bn_aggr(sv[:, 6:8], sv[:, 0:6]))(sv), reads=[stat_t[sti]], writes=[stat_t[sti]])
            P.op("act", (lambda sv: lambda e: e.activation(sv[:, 8:9], sv[:, 7:8], AF.Ln, bias=EPS))(sv), reads=[stat_t[sti]], writes=[stat_t[sti]])
            P.op("act", (lambda sv: lambda e: e.activation(sv[:, 9:10], sv[:, 8:9], AF.Exp, scale=-0.5))(sv), reads=[stat_t[sti]], writes=[stat_t[sti]])
            P.op("dve", (lambda vt, sv: lambda e: e.tensor_scalar(out=vt.f(), in0=vt.f(), scalar1=sv[:, 6:7], scalar2=sv[:, 9:10],
                                                                   op0=ALU.subtract, op1=ALU.mult))(vt, sv),
                 reads=vt.tk() + [stat_t[sti]], writes=vt.tk())
            for (dst, ig, ib) in ((vge, 0, 1), (vgo, 2, 3)):
                P.op("dve", (lambda dst, ig, vt: lambda e: e.tensor_tensor(out=dst.r(), in0=vt.f(), in1=lnv[:, ig], op=ALU.mult))(dst, ig, vt),
                     reads=vt.tk() + LNP.tk(), writes=dst.tk())
                P.op("dve", (lambda dst, ib: lambda e: e.tensor_tensor(out=dst.r(), in0=dst.f(), in1=lnv[:, ib], op=ALU.add))(dst, ib),
                     reads=LNP.tk() + dst.tk(), writes=dst.tk())
            if is_s:
                P.op("pool", (lambda vt, vge, vgo: lambda e: e.tensor_tensor(out=vt.f(), in0=vge.f(), in1=vgo.f(), op=ALU.add))(vt, vge, vgo),
                     reads=vge.tk() + vgo.tk(), writes=vt.tk())
                P.dma("gv", (lambda vt: lambda e: e.dma_start(out=gv_d[l], in_=vt.f()))(vt), reads=vt.tk())

        def s_part(b):
            is_s = has_s and b == nb - 1
            vge, vgo = VGE[b], VGO[b]
            wsx = (WSSM if (is_s and not os.environ.get("MK_T1")) else WSM).r().rearrange("p (g t) -> p g t", g=8)
            bo = 128 if (is_s and not os.environ.get("MK_T2")) else 0
            for cp in range(4):
                o_ap = BANK[bsg[cp]][:, b * 128:(b + 1) * 128]
                P.op("pe", (lambda o_ap, vge, cp, wsx: lambda e: e.matmul(o_ap, vge.r(cp * 128, (cp + 1) * 128), wsx[:, 2 * cp], start=True, stop=False))(o_ap, vge, cp, wsx),
                     reads=vge.tk() + WSM.tk() + WSSM.tk(), writes=[BANK_t[bsg[cp]]])
                P.op("pe", (lambda o_ap, vgo, cp, wsx: lambda e: e.matmul(o_ap, vgo.r(cp * 128, (cp + 1) * 128), wsx[:, 2 * cp + 1], start=False, stop=False))(o_ap, vgo, cp, wsx),
                     reads=vgo.tk() + WSM.tk() + WSSM.tk(), writes=[BANK_t[bsg[cp]]])
                P.op("pe", (lambda o_ap, cp, bo: lambda e: e.matmul(o_ap, gsel[:, cp, :], BSP.r(bo, bo + 128), start=False, stop=True))(o_ap, cp, bo),
                     reads=BSP.tk() + [con_t], writes=[BANK_t[bsg[cp]]])

        for b in range(nb):
            v_part(b)
        ln_a()
        ln_b(0, 4)
        for b in range(nb):
            s_part(b)
        stopchk("g2")
        for cp in range(4):
            P.op("dve", (lambda cp: lambda e: e.tensor_tensor(out=UB.r(cp * 512, cp * 512 + T), in0=UB.f(cp * 512, cp * 512 + T),
                                                               in1=BANK[bsg[cp]][:, 0:T], op=ALU.mult))(cp),
                 reads=UB.tk(cp * 512, cp * 512 + T) + [BANK_t[bsg[cp]]], writes=UB.tk(cp * 512, cp * 512 + T))
            bfree(bsg[cp])

        stopchk("s5_gmlp_done")
        tok0 = b0 * 128
        P.dma("cos", lambda e: e.dma_start(out=COS.f(0, T), in_=cos_d[:, tok0:tok0 + T]), writes=COS.tk())
        P.dma("sin", lambda e: e.dma_start(out=SIN.f(0, T), in_=sin_d[:, tok0:tok0 + T]), writes=SIN.tk())
        P.op("pool", lambda e: e.tensor_copy(VZ.r(), zcol.broadcast_to([128, 1280])), reads=[con_t], writes=VZ.tk())
        if not tile0:
            P.op("pool", lambda e: e.tensor_copy(VZ.r(0, 256), VZhalo[:, l].rearrange("p a b -> p (a b)")), reads=[halo_t[l][2]], writes=VZ.tk())
            P.op("pool", lambda e: e.tensor_copy(KT.r(0, 128), Khalo[:, l]), reads=[halo_t[l][1]], writes=KT.tk())

        stopchk("a0")

        QNB = [QN[0], Buf(12, 512)]
        kvw = {}
        st_ = {}

        def qk_A(i):
            if i == 0:
                w_, wt_ = wnext(4096)
                kvw["q"] = (w_[:, 0:4096].rearrange("p (k c) -> p k c", k=8), wt_)
            if i == 4:
                w_, wt_ = wnext(2048)
                kvw["kv"] = (w_[:, 0:2048].rearrange("p (k c) -> p k c", k=8), wt_)
            wv_, wt_ = kvw["q"] if i < 4 else kvw["kv"]
            c0_ = i * 128 if i < 4 else 0
            bq = balloc()
            for k in range(8):
                P.op("pe", (lambda bq, k, c0_, wv_: lambda e: e.matmul(BANK[bq][:, 0:T], wv_[:, k, c0_:c0_ + 128], hT[:, k, 0:T],
                                                                      start=(k == 0), stop=(k == 7)))(bq, k, c0_, wv_),
                     reads=[wt_, hT_t[k]], writes=[BANK_t[bq]])
            st_[i] = {"bq": bq}

        def qk_B(i):
            d = st_[i]
            bq = d["bq"]
            gcol = PC_QG if i < 4 else PC_KG
            t = ntmpr()
            qn = QNB[i % 2]
            P.op("act", (lambda t, bq: lambda e: e.activation(t.r(0, T), BANK[bq][:, 0:T], AF.Square))(t, bq), reads=[BANK_t[bq]], writes=t.tk())
            P.op("act", (lambda qn, bq, gcol: lambda e: e.activation(qn.r(0, T), BANK[bq][:, 0:T], AF.Copy, scale=par_c(l, gcol)))(qn, bq, gcol),
                 reads=[BANK_t[bq], par_t], writes=qn.tk())
            bfree(bq)
            bss = balloc()
            P.op("pe", (lambda bss, t: lambda e: e.matmul(BANK[bss][:, 0:T], bdR, t.r(0, T), start=True, stop=True))(bss, t), reads=t.tk() + [con_t], writes=[BANK_t[bss]])
            br = balloc()
            P.op("pe", (lambda br, qn: lambda e: e.matmul(BANK[br][:, 0:T], protR, qn.r(0, T), start=True, stop=True))(br, qn), reads=qn.tk() + [con_t], writes=[BANK_t[br]])
            d.update(bss=bss, br=br, qn=qn)

        def qk_C(i):
            d = st_[i]
            bss, br, qn = d["bss"], d["br"], d["qn"]
            if i < 4:
                npb_ = nb - 1 if has_s else nb
                dsts = [(QT.r().rearrange("p (b g t) -> p b g t", g=4, t=128)[:, 0:npb_, i, :], 0, npb_ * 128, 128)]
                if has_s:
                    dsts.append((QT.r((nb - 1) * 512, nb * 512).rearrange("p (s g q) -> p s g q", g=4, q=8)[:, :, i, :], Tp, T, 8))
                dst_tk = QT.tk()
            else:
                dsts = [(KT.r(128, 128 + T).rearrange("p (a b) -> p a b", b=128), 0, T, 128)]
                dst_tk = KT.tk()
            P.op("act", (lambda bss: lambda e: e.activation(R0.f(0, T), BANK[bss][:, 0:T], AF.Ln, bias=EPS, scale=1.0 / 64))(bss), reads=[BANK_t[bss]], writes=R0.tk())
            bfree(bss)
            P.op("act", lambda e: e.activation(R0.f(0, T), R0.f(0, T), AF.Exp, scale=-0.5), reads=R0.tk(), writes=R0.tk())
            t2 = QN[1]
            P.op("dve", (lambda qn: lambda e: e.tensor_tensor(out=t2.f(0, T), in0=qn.f(0, T), in1=COS.f(0, T), op=ALU.mult))(qn), reads=qn.tk() + COS.tk(), writes=t2.tk())
            t3 = ntmpf()
            P.op("dve", (lambda br, t3: lambda e: e.tensor_tensor(out=t3.f(0, T), in0=BANK[br][:, 0:T], in1=SIN.f(0, T), op=ALU.mult))(br, t3),
                 reads=[BANK_t[br]] + SIN.tk(), writes=t3.tk())
            bfree(br)
            P.op("dve", (lambda t3: lambda e: e.tensor_tensor(out=t2.f(0, T), in0=t2.f(0, T), in1=t3.f(0, T), op=ALU.add))(t3), reads=t2.tk() + t3.tk(), writes=t2.tk())
            for (dst_ap, lo, hi, shp) in dsts:
                P.op("dve", (lambda dst_ap, lo, hi, shp: lambda e: e.tensor_tensor(
                    out=dst_ap, in0=t2.f(lo, hi).rearrange("p (a b) -> p a b", b=shp), in1=R0.f(lo, hi).rearrange("p (a b) -> p a b", b=shp),
                    op=ALU.mult))(dst_ap, lo, hi, shp), reads=t2.tk() + R0.tk(), writes=dst_tk)

        qk_A(0)
        qk_A(1)
        qk_B(0)
        for i in range(5):
            if i + 2 < 5:
                qk_A(i + 2)
            if i + 1 < 5:
                qk_B(i + 1)
            qk_C(i)
        wv, wt = kvw["kv"]
        stopchk("a2")
        vzv = VZ.r().rearrange("p (b a d) -> p b a d", b=5, a=2)
        for b in range(nb):
            is_s = has_s and b == nb - 1
            bv = balloc()
            for k in range(8):
                P.op("pe", (lambda bv, k, b, wv: lambda e: e.matmul(BANK[bv][:, 0:128], hT[:, k, b * 128:(b + 1) * 128], wv[:, k, 128:256],
                                                                   start=(k == 0), stop=(k == 7)))(bv, k, b, wv),
                     reads=[wt, hT_t[k]], writes=[BANK_t[bv]])
            P.op("act", (lambda bv, b: lambda e: e.copy(vzv[:, 1 + b, 0, 0:64], BANK[bv][:, 0:64]))(bv, b), reads=[BANK_t[bv]], writes=VZ.tk())
            P.op("act", (lambda bv, b: lambda e: e.copy(vzv[:, 1 + b, 1, 64:128], BANK[bv][:, 64:128]))(bv, b), reads=[BANK_t[bv]], writes=VZ.tk())
            stopchk("av%d" % b)
            is_lastp = has_s and b == nb - 2
            if (is_s or is_lastp) and not os.environ.get("MK_T3"):
                vo_d = (vs_d if is_s else vp_d)
                P.dma("vout", (lambda b, vo_d: lambda e: [e.dma_start(out=vo_d[l][:, kv * 64:(kv + 1) * 64],
                                                                      in_=vzv[:, 1 + b, kv, kv * 64:(kv + 1) * 64].bitcast(F32)) for kv in range(2)])(b, vo_d),
                      reads=VZ.tk(), n=2)
                bt = balloc()
                P.op("pe", (lambda bt, b: lambda e: e.matmul(BANK[bt][:, 0:128], KT.r(128 + b * 128, 256 + b * 128), conR[:, CC_ID:CC_ID + 128],
                                                            start=True, stop=True))(bt, b),
                     reads=KT.tk() + [con_t], writes=[BANK_t[bt]])
                st2 = ntmpf()
                P.op("act", (lambda bt, st2: lambda e: e.copy(st2.f(0, 128), BANK[bt][:, 0:128]))(bt, st2), reads=[BANK_t[bt]], writes=st2.tk())
                bfree(bt)
                P.dma("kout", (lambda st2, is_s: lambda e: e.dma_start(out=(ks_d if is_s else kp_d)[l], in_=st2.f(0, 128)))(st2, is_s), reads=st2.tk())
            bfree(bv)
        stopchk("s6_attnproj_done")
        if not has_s:
            P.op("pool", lambda e: e.tensor_copy(Khalo[:, l], KT.r(nb * 128, (nb + 1) * 128)), reads=KT.tk(), writes=[halo_t[l][1]])
            P.op("pool", lambda e: e.tensor_copy(VZhalo[:, l].rearrange("p a b -> p (a b)"), VZ.r(nb * 256, (nb + 1) * 256)), reads=VZ.tk(), writes=[halo_t[l][2]])

        qv = QT.r().rearrange("p (g t) -> p g t", g=4)
        onv = ON.r().rearrange("p (g t) -> p g t", g=4)
        sk_b = sinkexp[:, l].unsqueeze(2).broadcast_to([128, 4, 128])

        def finish(bo_, bd_, col0, samp=False):
            t = ntmpf()
            if samp:
                P.op("dve", lambda e: e.tensor_tensor(out=t.f().rearrange("p (s g q) -> p s g q", g=4, q=8),
                                                      in0=BANK[bd_][:, :].rearrange("p (s g q) -> p s g q", g=4, q=8),
                                                      in1=sinkexp[:, l].unsqueeze(1).unsqueeze(3).broadcast_to([128, 16, 4, 8]), op=ALU.add),
                     reads=[BANK_t[bd_], sk_t], writes=t.tk())
            else:
                P.op("dve", lambda e: e.tensor_tensor(out=t.f().rearrange("p (g t) -> p g t", g=4), in0=BANK[bd_][:, :].rearrange("p (g t) -> p g t", g=4),
                                                      in1=sk_b, op=ALU.add), reads=[BANK_t[bd_], sk_t], writes=t.tk())
            bfree(bd_)
            P.op("act", lambda e: e.activation(t.f(), t.f(), AF.Ln), reads=t.tk(), writes=t.tk())
            P.op("act", lambda e: e.activation(t.f(), t.f(), AF.Exp, scale=-1.0), reads=t.tk(), writes=t.tk())
            if samp:
                P.op("dve", lambda e: e.tensor_tensor(out=onv[:, :, col0:col0 + 128].rearrange("p g (s q) -> p g s q", q=8),
                                                      in0=BANK[bo_][:, :].rearrange("p (s g q) -> p g s q", g=4, q=8),
                                                      in1=t.f().rearrange("p (s g q) -> p g s q", g=4, q=8), op=ALU.mult),
                     reads=[BANK_t[bo_]] + t.tk(), writes=ON.tk())
            else:
                P.op("dve", lambda e: e.tensor_tensor(out=onv[:, :, col0:col0 + 128], in0=BANK[bo_][:, :].rearrange("p (g t) -> p g t", g=4),
                                                      in1=t.f().rearrange("p (g t) -> p g t", g=4), op=ALU.mult),
                     reads=[BANK_t[bo_]] + t.tk(), writes=ON.tk())
            bfree(bo_)

        npb = nb - 1 if has_s else nb
        PTB = [Buf(16, 512), Buf(17, 512), Buf(18, 512), Buf(19, 512)]
        PTs = [PT, PT if has_s else PTB]

        def attn_s(b):
            has_prev = not (tile0 and b == 0)
            kbs = ([b] if has_prev else []) + [b + 1]
            plist = []
            for kv in range(2):
                for kb in kbs:
                    bs_ = balloc()
                    pt = PTs[b % 2][len(plist)]
                    P.op("pe", (lambda bs_, kv, kb, b: lambda e: e.matmul(BANK[bs_][:, :],
                                                                         KT.r(kb * 128, (kb + 1) * 128)[kv * 64:(kv + 1) * 64],
                                                                         QT.r(b * 512, (b + 1) * 512)[kv * 64:(kv + 1) * 64], start=True, stop=True))(bs_, kv, kb, b),
                         reads=KT.tk() + QT.tk(), writes=[BANK_t[bs_]])
                    P.op("act", (lambda bs_, pt: lambda e: e.activation(pt.r(), BANK[bs_][:, :], AF.Exp, scale=0.125))(bs_, pt),
                         reads=[BANK_t[bs_]], writes=pt.tk())
                    bfree(bs_)
                    mk = mcur if kb == b + 1 else mprev
                    P.op("dve", (lambda pt, mk: lambda e: e.tensor_tensor(out=pt.r().rearrange("p (g t) -> p g t", g=4),
                                                                            in0=pt.f().rearrange("p (g t) -> p g t", g=4),
                                                                            in1=mk.unsqueeze(1).broadcast_to([128, 4, 128]), op=ALU.mult))(pt, mk),
                         reads=pt.tk() + [con_t], writes=pt.tk())
                    plist.append((pt, kv, kb))
            return plist

        def attn_pv(b, plist):
            bo_, bd_ = balloc(), balloc()
            npl = len(plist)
            for i, (pt, kv, kb) in enumerate(plist):
                P.op("pe", (lambda pt, kv, kb, i, bo_, npl: lambda e: e.matmul(BANK[bo_][:, :], vzv[:, kb, kv, :], pt.r(), start=(i == 0), stop=(i == npl - 1)))(pt, kv, kb, i, bo_, npl),
                     reads=pt.tk() + VZ.tk(), writes=[BANK_t[bo_]])
                P.op("pe", (lambda pt, kv, i, bd_, npl: lambda e: e.matmul(BANK[bd_][:, :], oloR if kv == 0 else ohiR, pt.r(), start=(i == 0), stop=(i == npl - 1)))(pt, kv, i, bd_, npl),
                     reads=pt.tk() + [con_t], writes=[BANK_t[bd_]])
            finish(bo_, bd_, b * 128)

        if has_s:
            for b in range(npb):
                attn_pv(b, attn_s(b))
        else:
            pls = {0: attn_s(0)}
            for b in range(1, npb):
                pls[b] = attn_s(b)
                attn_pv(b - 1, pls[b - 1])
            attn_pv(npb - 1, pls[npb - 1])
        if has_s:
            sb = nb - 1
            c0 = sb * 128
            stopchk("s7_attnprompt_done")
            P.dma("kcr", lambda e: [e.dma_start(out=KCR.f(q * 512, (q + 1) * 512).rearrange("p (s d) -> p s d", s=4),
                                                in_=ck_d[l, q * 4:(q + 1) * 4].rearrange("s k d -> k s d")) for q in range(4)], writes=KCR.tk(), n=4)
            for q4 in range(4):
                bt = balloc()
                for s in range(4):
                    sq = q4 * 4 + s
                    P.op("pe", (lambda bt, s, sq: lambda e: e.transpose(BANK[bt][:, s * 128:(s + 1) * 128], KCR.f(sq * 128, (sq + 1) * 128), ident))(bt, s, sq),
                         reads=KCR.tk() + [con_t], writes=[BANK_t[bt]])
                P.op("act", (lambda bt, q4: lambda e: e.copy(KCT.r(q4 * 512, (q4 + 1) * 512), BANK[bt][:, :]))(bt, q4), reads=[BANK_t[bt]], writes=KCT.tk(q4 * 512, (q4 + 1) * 512))
                bfree(bt)
            stopchk("s7b_kctrans_done")
            bsp_ = [balloc(), balloc()]
            for kv in range(2):
                for s in range(16):
                    o_ap = BANK[bsp_[kv]][:, s * 32:(s + 1) * 32]
                    P.op("pe", (lambda o_ap, kv, s: lambda e: e.matmul(o_ap, KCT.r(s * 128, (s + 1) * 128)[kv * 64:(kv + 1) * 64],
                                                                      QT.r(sb * 512 + s * 32, sb * 512 + (s + 1) * 32)[kv * 64:(kv + 1) * 64],
                                                                      start=True, stop=True))(o_ap, kv, s),
                         reads=KCT.tk() + QT.tk(), writes=[BANK_t[bsp_[kv]]])
            for kv in range(2):
                pt = PT[kv]
                P.op("act", (lambda kv, pt: lambda e: e.activation(pt.r(), BANK[bsp_[kv]][:, :], AF.Exp, scale=0.125))(kv, pt),
                     reads=[BANK_t[bsp_[kv]]], writes=pt.tk())
                bfree(bsp_[kv])
                P.op("dve", (lambda pt: lambda e: e.tensor_tensor(out=pt.r().rearrange("p (a q) -> p a q", q=8), in0=pt.f().rearrange("p (a q) -> p a q", q=8),
                                                                    in1=mpast.unsqueeze(1).broadcast_to([128, 64, 8]), op=ALU.mult))(pt),
                     reads=pt.tk() + [con_t], writes=pt.tk())
            stopchk("s8a_pastscores_done")
            for kv in range(2):
                bs_ = balloc()
                pt = PT[2 + kv]
                P.op("pe", (lambda bs_, kv: lambda e: e.matmul(BANK[bs_][:, :],
                                                              KT.r(128 + c0, 256 + c0)[kv * 64:(kv + 1) * 64],
                                                              QT.r(sb * 512, (sb + 1) * 512)[kv * 64:(kv + 1) * 64], start=True, stop=True))(bs_, kv),
                     reads=KT.tk() + QT.tk(), writes=[BANK_t[bs_]])
                P.op("act", (lambda bs_, pt: lambda e: e.activation(pt.r(), BANK[bs_][:, :], AF.Exp, scale=0.125))(bs_, pt), reads=[BANK_t[bs_]], writes=pt.tk())
                bfree(bs_)
                P.op("dve", (lambda pt: lambda e: e.tensor_tensor(out=pt.r().rearrange("p (s g q) -> p s g q", g=4, q=8),
                                                                    in0=pt.f().rearrange("p (s g q) -> p s g q", g=4, q=8),
                                                                    in1=mnew.rearrange("p (s q) -> p s q", q=8).unsqueeze(2).broadcast_to([128, 16, 4, 8]), op=ALU.mult))(pt),
                     reads=pt.tk() + [con_t], writes=pt.tk())
            stopchk("s8_samplescores_done")
            bo_, bd_, bop, bdp = balloc(), balloc(), balloc(), balloc()
            for kv in range(2):
                P.op("pe", (lambda kv: lambda e: e.matmul(BANK[bo_][:, :], vzv[:, 1 + sb, kv, :], PT[2 + kv].r(), start=(kv == 0), stop=(kv == 1)))(kv),
                     reads=PT[2 + kv].tk() + VZ.tk(), writes=[BANK_t[bo_]])
                P.op("pe", (lambda kv: lambda e: e.matmul(BANK[bd_][:, :], oloR if kv == 0 else ohiR, PT[2 + kv].r(), start=(kv == 0), stop=(kv == 1)))(kv),
                     reads=PT[2 + kv].tk() + [con_t], writes=[BANK_t[bd_]])
            vzcv = VZC.r().rearrange("p (s a d) -> p s a d", s=8, a=2)
            for rnd in range(2):
                P.op("pool", lambda e: e.tensor_copy(VZC.r(), zcol.broadcast_to([128, 2048])), reads=[con_t], writes=VZC.tk())
                P.dma("vzc", (lambda rnd: lambda e: [e.dma_start(out=vzcv[:, :, kv, kv * 64:(kv + 1) * 64],
                                                                 in_=cv_d[l, rnd * 8:(rnd + 1) * 8, :, kv * 64:(kv + 1) * 64].rearrange("s k d -> k s d"))
                                                    for kv in range(2)])(rnd), writes=VZC.tk(), n=2, eng="pool")
                for s8 in range(8):
                    s = rnd * 8 + s8
                    for kv in range(2):
                        p_ap = PT[kv].r(s * 32, (s + 1) * 32)
                        P.op("pe", (lambda p_ap, s8, kv, s: lambda e: e.matmul(
                            BANK[bop][:, s * 32:(s + 1) * 32], vzcv[:, s8, kv, :], p_ap,
                            start=(kv == 0), stop=(kv == 1)))(p_ap, s8, kv, s),
                            reads=PT[kv].tk() + VZC.tk(), writes=[BANK_t[bop]])
                    for kv in range(2):
                        p_ap = PT[kv].r(s * 32, (s + 1) * 32)
                        P.op("pe", (lambda p_ap, kv, s: lambda e: e.matmul(
                            BANK[bdp][:, s * 32:(s + 1) * 32], oloR if kv == 0 else ohiR, p_ap,
                            start=(kv == 0), stop=(kv == 1)))(p_ap, kv, s),
                            reads=PT[kv].tk() + [con_t], writes=[BANK_t[bdp]])
            to_ = QN[1]
            P.op("act", lambda e: e.copy(to_.f(), BANK[bop][:, :]), reads=[BANK_t[bop]], writes=to_.tk())
            bfree(bop)
            td_ = R2
            P.op("act", lambda e: e.copy(td_.f(), BANK[bdp][:, :]), reads=[BANK_t[bdp]], writes=td_.tk())
            bfree(bdp)
            P.op("dve", lambda e: e.tensor_tensor(out=to_.f(), in0=to_.f(), in1=BANK[bo_][:, :], op=ALU.add), reads=to_.tk() + [BANK_t[bo_]], writes=to_.tk())
            P.op("dve", lambda e: e.tensor_tensor(out=td_.f(), in0=td_.f(), in1=BANK[bd_][:, :], op=ALU.add), reads=td_.tk() + [BANK_t[bd_]], writes=td_.tk())
            bfree(bo_)
            bfree(bd_)
            t_ = ntmpf()
            P.op("dve", lambda e: e.tensor_tensor(out=t_.f().rearrange("p (s g q) -> p s g q", g=4, q=8),
                                                  in0=td_.f().rearrange("p (s g q) -> p s g q", g=4, q=8),
                                                  in1=sinkexp[:, l].unsqueeze(1).unsqueeze(3).broadcast_to([128, 16, 4, 8]), op=ALU.add),
                 reads=td_.tk() + [sk_t], writes=t_.tk())
            P.op("act", lambda e: e.activation(t_.f(), t_.f(), AF.Ln), reads=t_.tk(), writes=t_.tk())
            P.op("act", lambda e: e.activation(t_.f(), t_.f(), AF.Exp, scale=-1.0), reads=t_.tk(), writes=t_.tk())
            P.op("dve", lambda e: e.tensor_tensor(out=onv[:, :, c0:c0 + 128].rearrange("p g (s q) -> p g s q", q=8),
                                                  in0=to_.f().rearrange("p (s g q) -> p g s q", g=4, q=8),
                                                  in1=t_.f().rearrange("p (s g q) -> p g s q", g=4, q=8), op=ALU.mult),
                 reads=to_.tk() + t_.tk(), writes=ON.tk())

        if DEBUG and l == 0 and ti == 0:
            for i_, bf_ in enumerate((SA, UB, ON, QT)):
                P.dma("dbg%d" % i_, (lambda i_, bf_: lambda e: e.dma_start(out=dbg_d[i_], in_=bf_.f()))(i_, bf_), reads=bf_.tk())
            P.dma("dbg4", lambda e: e.dma_start(out=dbg_d[4, :, 0:1024], in_=KT.f()), reads=KT.tk())
            P.dma("dbg5", lambda e: e.dma_start(out=dbg_d[5, :, 0:1280], in_=VZ.f()), reads=VZ.tk())
        stopchk("s9_attnsample_done")
        srcs = (SA, UB, ON)
        for j in range(8):
            wo, wot = wnext(1536)
            wov = wo[:, 0:1536].rearrange("p (b k c) -> p b k c", b=3, k=4)
            bys = []
            for br in range(3):
                by = balloc()
                bys.append(by)
                src = srcs[br]
                for kc in range(4):
                    P.op("pe", (lambda by, kc, br, wov, src: lambda e: e.matmul(BANK[by][:, 0:T], wov[:, br, kc, :], src.r(kc * 512, kc * 512 + T),
                                                                               start=(kc == 0), stop=(kc == 3)))(by, kc, br, wov, src),
                         reads=[wot] + src.tk(kc * 512, kc * 512 + T), writes=[BANK_t[by]])
            wg, wgt = wnext(3072)
            wgv = wg[:, 0:3072].rearrange("p (k b c) -> p k b c", k=8, b=3)
            for br in range(3):
                bg = balloc()
                by = bys[br]
                for k in range(8):
                    P.op("pe", (lambda bg, k, br, wgv: lambda e: e.matmul(BANK[bg][:, 0:T], wgv[:, k, br, :], hT[:, k, 0:T], start=(k == 0), stop=(k == 7)))(bg, k, br, wgv),
                         reads=[wgt, hT_t[k]], writes=[BANK_t[bg]])
                t = ntmpf()
                P.op("act", (lambda t, bg: lambda e: e.activation(t.f(0, T), BANK[bg][:, 0:T], AF.Sigmoid))(t, bg), reads=[BANK_t[bg]], writes=t.tk())
                bfree(bg)
                mj = (j * 512, j * 512 + T)
                if br == 0:
                    P.op("dve", (lambda t, by, mj: lambda e: e.tensor_tensor(out=MIX.r(*mj), in0=t.f(0, T), in1=BANK[by][:, 0:T], op=ALU.mult))(t, by, mj),
                         reads=t.tk() + [BANK_t[by]], writes=MIX.tk(*mj))
                else:
                    P.op("dve", (lambda t, by: lambda e: e.tensor_tensor(out=t.f(0, T), in0=t.f(0, T), in1=BANK[by][:, 0:T], op=ALU.mult))(t, by),
                         reads=t.tk() + [BANK_t[by]], writes=t.tk())
                    P.op("dve", (lambda t, mj: lambda e: e.tensor_tensor(out=MIX.r(*mj), in0=MIX.f(*mj), in1=t.f(0, T), op=ALU.add))(t, mj),
                         reads=t.tk() + MIX.tk(*mj), writes=MIX.tk(*mj))
                bfree(by)
        for dp in range(4):
            w, wt = wnext(2048)
            wv = w[:, 0:2048].rearrange("p (k c) -> p k c", k=8)
            for d in range(2):
                by = balloc()
                for k in range(8):
                    P.op("pe", (lambda by, k, d, wv: lambda e: e.matmul(BANK[by][:, 0:T], wv[:, k, d * 128:(d + 1) * 128], MIX.r(k * 512, k * 512 + T),
                                                                       start=(k == 0), stop=(k == 7)))(by, k, d, wv),
                         reads=[wt] + MIX.tk(k * 512, k * 512 + T), writes=[BANK_t[by]])
                kk = dp * 2 + d
                P.op("dve", (lambda by, kk: lambda e: e.tensor_tensor(out=xT[:, kk, 0:T], in0=BANK[by][:, 0:T], in1=xT[:, kk, 0:T], op=ALU.add))(by, kk),
                     reads=[BANK_t[by], xT_t[kk]], writes=[xT_t[kk]])
                bfree(by)


    stopped = False
    try:
        for ti, (b0, nb) in enumerate(tiles):
            T = nb * 128
            has_s = (b0 + nb == NBLK)
            Tp = T - 128 if has_s else T
            for b in range(nb):
                sg = STG[b % 2]
                r0 = (b0 + b) * 128
                P.dma("xin%d" % (b % 2), (lambda sg, r0: lambda e: e.dma_start(out=sg.f(), in_=xin_d[r0:r0 + 128, :]))(sg, r0), writes=sg.tk())
                for hh in range(2):
                    bt = balloc()
                    for k4 in range(4):
                        k = hh * 4 + k4
                        P.op("pe", (lambda bt, k4, k, sg: lambda e: e.transpose(BANK[bt][:, k4 * 128:(k4 + 1) * 128], sg.f(k * 128, (k + 1) * 128), ident))(bt, k4, k, sg),
                             reads=sg.tk() + [con_t], writes=[BANK_t[bt]])
                    P.op("act" if hh == 0 else "dve",
                         (lambda bt, hh, b: lambda e: (e.copy if hh == 0 else e.tensor_copy)(xT[:, hh * 4:(hh + 1) * 4, b * 128:(b + 1) * 128],
                                                                                             BANK[bt][:, :].rearrange("p (k t) -> p k t", k=4)))(bt, hh, b),
                         reads=[BANK_t[bt]], writes=xT_t[hh * 4:(hh + 1) * 4])
                    bfree(bt)
            for l in range(depth):
                rmsnorm(l, PC_N1, T)
                ffn(l, T)
                rmsnorm(l, PC_NM, T)
                mixer(l, ti, b0, nb, T, Tp, has_s)
                rmsnorm(l, PC_N2, T)
                ffn(l, T)
            for b in range(nb):
                sg = STG[b % 2]
                r0 = (b0 + b) * 128
                for hh in range(2):
                    bt = balloc()
                    for k4 in range(4):
                        k = hh * 4 + k4
                        P.op("pe", (lambda bt, k4, k, b: lambda e: e.transpose(BANK[bt][:, k4 * 128:(k4 + 1) * 128], xT[:, k, b * 128:(b + 1) * 128], ident))(bt, k4, k, b),
                             reads=[xT_t[k], con_t], writes=[BANK_t[bt]])
                    P.op("act" if hh == 0 else "dve",
                         (lambda bt, hh, sg: lambda e: (e.copy if hh == 0 else e.tensor_copy)(sg.f(hh * 512, (hh + 1) * 512), BANK[bt][:, :]))(bt, hh, sg),
                         reads=[BANK_t[bt]], writes=sg.tk(hh * 512, (hh + 1) * 512))
                    bfree(bt)
                P.dma("yout%d" % (b % 2), (lambda sg, r0: lambda e: e.dma_start(out=y_d[r0:r0 + 128, :], in_=sg.f()))(sg, r0), reads=sg.tk())
    except StopBuild:
        stopped = True
    assert stopped or wstate["next"] == len(wlist)
    P.emit()
    P.close()
    return nc


_CACHE = {}


def make_in_maps(inp):
    W = pack_weights(inp)
    par = pack_params(inp)
    con = make_consts()
    gsel = make_gsel()
    lnp = np.zeros((L, 4, 512), np.float32)
    ev = (np.arange(512) // 64) % 2 == 0
    for l in range(L):
        lnp[l, 0, ev] = inp["ln_gmlp_g"][l][ev]
        lnp[l, 1, ev] = inp["ln_gmlp_b"][l][ev]
        lnp[l, 2, ~ev] = inp["ln_gmlp_g"][l][~ev]
        lnp[l, 3, ~ev] = inp["ln_gmlp_b"][l][~ev]
    wsT = np.ascontiguousarray(inp["w_spatial"].transpose(0, 3, 1, 2))
    wsS = np.zeros((L, 128, 8, 128), np.float32)
    for s in range(16):
        wsS[:, s * 8:(s + 1) * 8, :, s * 8:(s + 1) * 8] = inp["w_spatial"][:, :, :8, :8].transpose(0, 3, 1, 2)
    bsp = np.zeros((L, 128, 256), np.float32)
    bsp[:, 0:8, 0:128] = inp["b_spatial"]
    bsp[:, 0:8, 128:256] = np.tile(inp["b_spatial"][:, :, :8], (1, 1, 16))
    in_maps = []
    for c in range(NCORE):
        b, half = c // 2, c % 2
        start = 0 if half == 0 else B_START
        cs, sn = rope_tables(start)
        xin = np.concatenate([inp["x_prompt"][b, start:start + A_LEN], inp["x_sample"][16 * c:16 * c + 16].reshape(128, D)], axis=0)
        in_maps.append({
            "xin": np.ascontiguousarray(xin), "wl": W, "par": par, "con": con, "gsel": gsel, "cos": cs, "sin": sn,
            "lnp": lnp, "wsT": wsT, "wsS": wsS, "bsp": bsp,
            "stc": np.ascontiguousarray(inp["state_conv"][:, 16 * c:16 * c + 16]),
            "ck": np.ascontiguousarray(inp["cache_k"][:, 16 * c:16 * c + 16].reshape(L, 16, 128, 128)),
            "cv": np.ascontiguousarray(inp["cache_v"][:, 16 * c:16 * c + 16].reshape(L, 16, 128, 128)),
        })
    return in_maps


def kernel(**inputs):
    inp = {k: np.asarray(v) for k, v in inputs.items()}
    if "nc" not in _CACHE:
        _CACHE["nc"] = build()
    nc = _CACHE["nc"]
    in_maps = make_in_maps(inp)
    res = run_bass_kernel_spmd(nc, in_maps, core_ids=list(range(NCORE)))
    R = res.results
    y_prompt = np.empty((4, 4096, D), np.float32)
    y_sample = np.empty((128, 8, D), np.float32)
    conv_p = np.empty((L, 4, 30, 512), np.float32)
    conv_s = np.empty((L, 128, 30, 512), np.float32)
    k_p = np.empty((L, 4, 128, 2, 64), np.float32)
    v_p = np.empty((L, 4, 128, 2, 64), np.float32)
    k_s = np.empty((L, 128, 8, 2, 64), np.float32)
    v_s = np.empty((L, 128, 8, 2, 64), np.float32)
    gv_s = np.empty((L, 128, 8, 512), np.float32)
    for c in range(NCORE):
        b, half = c // 2, c % 2
        r = R[c]
        y = r["y"]
        if half == 0:
            y_prompt[b, 0:A_LEN] = y[0:A_LEN]
        else:
            y_prompt[b, A_LEN:4096] = y[A_LEN - (4096 - A_LEN):A_LEN]
            conv_p[:, b] = r["convp"]
            k_p[:, b] = r["kp"].reshape(L, 128, 2, 64)
            v_p[:, b] = r["vp"].reshape(L, 128, 2, 64)
        y_sample[16 * c:16 * c + 16] = y[A_LEN:].reshape(16, 8, D)
        conv_s[:, 16 * c:16 * c + 16] = r["convs"]
        k_s[:, 16 * c:16 * c + 16] = r["ks"].reshape(L, 16, 8, 2, 64)
        v_s[:, 16 * c:16 * c + 16] = r["vs"].reshape(L, 16, 8, 2, 64)
        gv_s[:, 16 * c:16 * c + 16] = r["gv"].reshape(L, 16, 8, 512)
    return (y_prompt, y_sample, conv_p, conv_s, k_p, v_p, k_s, v_s, gv_s)
```

```python
import os
from contextlib import ExitStack
import numpy as np
import concourse.bass as bass
import concourse.mybir as mybir
from concourse.bass_utils import run_bass_kernel_spmd

F32 = mybir.dt.float32
F32R = mybir.dt.float32r
AF = mybir.ActivationFunctionType
ALU = mybir.AluOpType

ENGS = ["pe", "act", "dve", "pool", "sp"]
L = 4
D = 1024
DFF = 2816
NCORE = 8
NPB = 18
NBLK = 19
A_LEN = NPB * 128
B_START = 4096 - A_LEN
EPS = 1e-6


class Tok:
    __slots__ = ("w", "r")

    def __init__(self):
        self.w = None
        self.r = []


class Op:
    __slots__ = ("eng", "fn", "deps", "is_dma", "key", "cnt", "need_inc", "n_instr")

    def __init__(self, eng, fn):
        self.eng = eng
        self.fn = fn
        self.deps = ()
        self.is_dma = False
        self.key = None
        self.cnt = 0
        self.need_inc = False
        self.n_instr = 1


class Prog:
    def __init__(self, nc):
        self.nc = nc
        self.ops = {e: [] for e in ENGS}
        self.all_ops = []
        self.dma_keys = {}
        self.stack = ExitStack()

    def sbuf(self, name, shape, dtype):
        return self.stack.enter_context(self.nc.sbuf_tensor("sb_" + name, shape, dtype))

    def psum(self, name, shape, dtype):
        return self.stack.enter_context(self.nc.psum_tensor("ps_" + name, shape, dtype))

    def op(self, eng, fn, reads=(), writes=()):
        o = Op(eng, fn)
        deps = set()
        for t in reads:
            if t.w is not None:
                deps.add(t.w)
        for t in writes:
            if t.w is not None:
                deps.add(t.w)
            deps.update(t.r)
        o.deps = deps
        for t in reads:
            t.r.append(o)
        for t in writes:
            t.w = o
            t.r = []
        self.ops[eng].append(o)
        self.all_ops.append(o)
        return o

    def dma(self, key, fn, reads=(), writes=(), n=1, eng="sp"):
        o = self.op(eng, fn, reads, writes)
        o.is_dma = True
        o.key = key
        o.n_instr = n
        ent = self.dma_keys.setdefault(key, [0, None])
        if ent[1] is not None:
            o.deps.add(ent[1])
        ent[0] += 16 * n
        ent[1] = o
        o.cnt = ent[0]
        return o

    def emit(self):
        nc = self.nc
        for o in self.all_ops:
            for d in o.deps:
                if not d.is_dma:
                    if d.eng == "pe" and o.eng == "pe" and not o.is_dma:
                        continue
                    d.need_inc = True
        for e in ENGS:
            c = 0
            for o in self.ops[e]:
                if o.is_dma:
                    continue
                if o.need_inc:
                    c += 1
                    o.cnt = c
        st = self.stack
        esem = {e: st.enter_context(nc.semaphore("s_" + e)) for e in ENGS}
        ksem = {k: st.enter_context(nc.semaphore("k_%d" % i)) for i, k in enumerate(self.dma_keys)}
        block = st.enter_context(nc.Block())

        def run(e, h):
            waited = {}
            for o in self.ops[e]:
                need = {}
                for d in o.deps:
                    if d.is_dma:
                        s = ksem[d.key]
                        sk = ("k", d.key)
                    else:
                        if d.eng == "pe" and e == "pe" and not o.is_dma:
                            continue
                        s = esem[d.eng]
                        sk = ("e", d.eng)
                    if d.cnt > waited.get(sk, 0) and d.cnt > need.get(sk, (None, 0))[1]:
                        need[sk] = (s, d.cnt)
                for sk, (s, v) in need.items():
                    h.wait_ge(s, v)
                    waited[sk] = v
                r = o.fn(h)
                if o.is_dma:
                    rs = r if isinstance(r, (list, tuple)) else [r]
                    assert len(rs) == o.n_instr
                    for i in rs:
                        i.then_inc(ksem[o.key], 16)
                elif o.need_inc:
                    r.then_inc(esem[e], 1)
            if e == "sp":
                for k, ent in self.dma_keys.items():
                    if ent[0] > waited.get(("k", k), 0):
                        h.wait_ge(ksem[k], ent[0])

        @block.tensor
        def _(h):
            run("pe", h)

        @block.scalar
        def _(h):
            run("act", h)

        @block.vector
        def _(h):
            run("dve", h)

        @block.gpsimd
        def _(h):
            run("pool", h)

        @block.sync
        def _(h):
            run("sp", h)

    def close(self):
        self.stack.close()


WSLOT = 4096
NWS = 3


def layer_schedule():
    s = []
    for f in (1, 2):
        for hf in range(2):
            for j in range(11):
                s.append(("gu", f, hf * 11 + j, 2048))
            for dp in range(4):
                s.append(("wd", f, hf, dp, 2816))
        if f == 1:
            for c in range(4):
                s.append(("glu", c, 2048))
            s.append(("u", 4096))
            s.append(("vg", 4096))
            s.append(("q", 4096))
            s.append(("kv", 2048))
            for j in range(8):
                s.append(("oproj", j, 1536))
                s.append(("gate", j, 3072))
            for dp in range(4):
                s.append(("wout", dp, 2048))
    return s


SCHED = layer_schedule()
WCOLS = sum(e[-1] for e in SCHED)

PC_N1, PC_NM, PC_N2 = 0, 8, 16
PC_WDW = 24
PC_BC = 148
PC_LG = 152
PC_LB = 156
PC_QG = 160
PC_KG = 161
PC_SK = 162
NPAR = 166

CC_ID = 0
CC_ONES = 128
CC_BD = 256
CC_OLO = 384
CC_OHI = 512
CC_PROT = 640
CC_MCUR = 768
CC_MPREV = 896
CC_MNEW = 1024
CC_MPAST = 1152
NCONST = 1160


def q_perm():
    idx = []
    for g in range(4):
        idx += list(range(g * 64, g * 64 + 64)) + list(range((4 + g) * 64, (4 + g) * 64 + 64))
    return np.array(idx)


def pack_weights(inp):
    W = np.empty((L, 128, WCOLS), np.float32)
    qp = q_perm()
    for l in range(L):
        off = 0
        win = inp["w_in"][l]
        for e in SCHED:
            n = e[-1]
            if e[0] == "gu":
                w = (inp["w_ffn1_gu"] if e[1] == 1 else inp["w_ffn2_gu"])[l]
                j = e[2]
                g = w[:, j * 128:(j + 1) * 128].reshape(8, 128, 128)
                u = w[:, DFF + j * 128:DFF + (j + 1) * 128].reshape(8, 128, 128)
                t = np.concatenate([g, u], axis=2).transpose(1, 0, 2)
            elif e[0] == "wd":
                w = (inp["w_ffn1_down"] if e[1] == 1 else inp["w_ffn2_down"])[l]
                hf, dp = e[2], e[3]
                t = w[hf * 11 * 128:(hf + 1) * 11 * 128, dp * 256:(dp + 1) * 256].reshape(11, 128, 256).transpose(1, 0, 2)
            elif e[0] == "glu":
                c = e[1]
                a = win[:, c * 128:(c + 1) * 128].reshape(8, 128, 128)
                g = win[:, 512 + c * 128:512 + (c + 1) * 128].reshape(8, 128, 128)
                t = np.concatenate([a, g], axis=2).transpose(1, 0, 2)
            elif e[0] == "u":
                t = win[:, 1024:1536].reshape(8, 128, 512).transpose(1, 0, 2)
            elif e[0] == "vg":
                t = win[:, 1536:2048].reshape(8, 128, 512).transpose(1, 0, 2)
            elif e[0] == "q":
                t = win[:, 2048:2560][:, qp].reshape(8, 128, 512).transpose(1, 0, 2)
            elif e[0] == "kv":
                t = win[:, 2560:2816].reshape(8, 128, 256).transpose(1, 0, 2)
            elif e[0] == "gate":
                j = e[1]
                gs = [win[:, 2816 + br * 1024 + j * 128:2816 + br * 1024 + (j + 1) * 128].reshape(8, 128, 1, 128)
                      for br in range(3)]
                t = np.concatenate(gs, axis=2).transpose(1, 0, 2, 3)
            elif e[0] == "oproj":
                j = e[1]
                wo = inp["w_o"][l][qp]
                ps = [m[:, j * 128:(j + 1) * 128].reshape(1, 4, 128, 128)
                      for m in (inp["w_conv_out"][l], inp["w_gmlp_out"][l], wo)]
                t = np.concatenate(ps, axis=0).transpose(2, 0, 1, 3)
            elif e[0] == "wout":
                dp = e[1]
                t = inp["w_out"][l][:, dp * 256:(dp + 1) * 256].reshape(8, 128, 256).transpose(1, 0, 2)
            W[l, :, off:off + n] = t.reshape(128, n)
            off += n
    return W


def pack_params(inp):
    Pm = np.zeros((128, L, NPAR), np.float32)
    for l in range(L):
        for nm, c0 in (("norm_ffn1", PC_N1), ("norm_mix", PC_NM), ("norm_ffn2", PC_N2)):
            Pm[:, l, c0:c0 + 8] = inp[nm][l].reshape(8, 128).T
        Pm[:, l, PC_WDW:PC_WDW + 124] = inp["w_conv_dw"][l].reshape(31, 4, 128).transpose(2, 1, 0).reshape(128, 124)
        Pm[:, l, PC_BC:PC_BC + 4] = inp["b_conv_dw"][l].reshape(4, 128).T
        Pm[:, l, PC_LG:PC_LG + 4] = inp["ln_conv_g"][l].reshape(4, 128).T
        Pm[:, l, PC_LB:PC_LB + 4] = inp["ln_conv_b"][l].reshape(4, 128).T
        Pm[:, l, PC_QG] = np.tile(inp["q_norm_g"][l], 2)
        Pm[:, l, PC_KG] = np.tile(inp["k_norm_g"][l], 2)
        sk = inp["attn_sinks"][l]
        Pm[:64, l, PC_SK:PC_SK + 4] = sk[None, 0:4]
        Pm[64:, l, PC_SK:PC_SK + 4] = sk[None, 4:8]
    return Pm


def make_consts():
    C = np.zeros((128, NCONST), np.float32)
    C[:, CC_ID:CC_ID + 128] = np.eye(128)
    C[:, CC_ONES:CC_ONES + 128] = 1.0
    C[:64, CC_BD:CC_BD + 64] = 1.0
    C[64:, CC_BD + 64:CC_BD + 128] = 1.0
    C[:, CC_OLO:CC_OLO + 64] = 1.0
    C[:, CC_OHI + 64:CC_OHI + 128] = 1.0
    for m in range(128):
        if m % 64 < 32:
            C[m + 32, CC_PROT + m] = -1.0
        else:
            C[m - 32, CC_PROT + m] = 1.0
    s = np.arange(128)[:, None]
    q = np.arange(128)[None, :]
    C[:, CC_MCUR:CC_MCUR + 128] = (s <= q)
    C[:, CC_MPREV:CC_MPREV + 128] = (s >= q)
    C[:, CC_MNEW:CC_MNEW + 128] = ((s // 8) == (q // 8)) & ((s % 8) <= (q % 8))
    C[:, CC_MPAST:CC_MPAST + 8] = (np.arange(128)[:, None] >= np.arange(8)[None, :])
    return C


def make_gsel():
    G = np.zeros((128, 4, 128), np.float32)
    for cp in range(4):
        G[2 * cp, cp, :64] = 1.0
        G[2 * cp + 1, cp, 64:] = 1.0
    return G


def rope_tables(start):
    half = 32
    inv_freq = (1.0 / (10000.0 ** (np.arange(half, dtype=np.float32) / half))).astype(np.float32)
    pos = np.concatenate([start + np.arange(A_LEN), 8192 + (np.arange(128) % 8)]).astype(np.float32)
    ang = pos[None, :] * inv_freq[np.arange(128) % 32][:, None]
    return np.cos(ang).astype(np.float32), np.sin(ang).astype(np.float32)


def build(depth=L, tiles=None):
    if tiles is None:
        tiles = [(0, 4), (4, 4), (8, 4), (12, 4), (16, 3)]
    nc = bass.Bass("TRN2", target_bir_lowering=False)
    dt_in = lambda n, s: nc.dram_tensor(n, s, F32, kind="ExternalInput").ap()
    dt_out = lambda n, s: nc.dram_tensor(n, s, F32, kind="ExternalOutput").ap()
    xin_d = dt_in("xin", [NBLK * 128, D])
    wl_d = dt_in("wl", [L, 128, WCOLS])
    par_d = dt_in("par", [128, L, NPAR])
    con_d = dt_in("con", [128, NCONST])
    gsel_d = dt_in("gsel", [128, 4, 128])
    cos_d = dt_in("cos", [128, NBLK * 128])
    sin_d = dt_in("sin", [128, NBLK * 128])
    lnp_d = dt_in("lnp", [L, 4, 512])
    wsT_d = dt_in("wsT", [L, 128, 8, 128])
    wsS_d = dt_in("wsS", [L, 128, 8, 128])
    bsp_d = dt_in("bsp", [L, 128, 256])
    stc_d = dt_in("stc", [L, 16, 30, 512])
    ck_d = dt_in("ck", [L, 16, 128, 128])
    cv_d = dt_in("cv", [L, 16, 128, 128])
    y_d = dt_out("y", [NBLK * 128, D])
    convp_d = dt_out("convp", [L, 30, 512])
    convs_d = dt_out("convs", [L, 16, 30, 512])
    kp_d = dt_out("kp", [L, 128, 128])
    vp_d = dt_out("vp", [L, 128, 128])
    ks_d = dt_out("ks", [L, 128, 128])
    vs_d = dt_out("vs", [L, 128, 128])
    gv_d = dt_out("gv", [L, 128, 512])
    DEBUG = bool(int(os.environ.get("MK_DEBUG", "0")))
    if DEBUG:
        dbg_d = dt_out("dbgo", [6, 128, 2048])

    P = Prog(nc)
    SKIP = os.environ.get("MK_SKIP", "").split(",")
    _dma = P.dma

    def dma_f(key, fn, reads=(), writes=(), n=1, eng="sp"):
        if key in SKIP:
            return None
        return _dma(key, fn, reads=reads, writes=writes, n=n, eng=eng)
    P.dma = dma_f
    WS = [P.sbuf("ws%d" % i, [128, WSLOT], F32R) for i in range(NWS)]
    WS_t = [Tok() for _ in range(NWS)]
    xT = P.sbuf("xT", [128, 8, 512], F32)
    xT_t = [Tok() for _ in range(8)]
    hT = P.sbuf("hT", [128, 8, 512], F32R)
    hT_t = [Tok() for _ in range(8)]
    par = P.sbuf("par", [128, L, NPAR], F32)
    par_t = Tok()
    con = P.sbuf("con", [128, NCONST], F32)
    conR = P.sbuf("conR", [128, NCONST], F32R)
    con_t = Tok()
    gsel = P.sbuf("gsel", [128, 4, 128], F32R)
    sinkexp = P.sbuf("sinkexp", [128, L, 4], F32)
    sk_t = Tok()
    Atail = P.sbuf("Atail", [128, L, 4, 30], F32R)
    Khalo = P.sbuf("Khalo", [128, L, 128], F32R)
    VZhalo = P.sbuf("VZhalo", [128, L, 2, 128], F32R)
    halo_t = [[Tok() for _ in range(3)] for _ in range(L)]
    stat = P.sbuf("stat", [128, 2, 12], F32)
    stat_t = [Tok(), Tok()]
    NPR, NPF = 36, 15
    SCR = P.sbuf("scr", [128, NPR * 512], F32R)
    SCF = P.sbuf("scf", [128, NPF * 512], F32)
    PGR_t = [Tok() for _ in range(NPR)]
    PGF_t = [Tok() for _ in range(NPF)]
    BANK = [P.psum("bank%d" % i, [128, 512], F32) for i in range(8)]
    BANK_t = [Tok() for _ in range(8)]
    bank_free = list(range(8))

    def balloc():
        return bank_free.pop(0)

    def bfree(b):
        bank_free.append(b)

    class Buf:
        def __init__(self, pg0, ncols, sp="R"):
            self.c0 = pg0 * 512
            self.n = ncols
            self.sp = sp

        def r(self, lo=0, hi=None):
            assert self.sp == "R"
            hi = self.n if hi is None else hi
            return SCR[:, self.c0 + lo:self.c0 + hi]

        def f(self, lo=0, hi=None):
            hi = self.n if hi is None else hi
            if self.sp == "R":
                return SCR[:, self.c0 + lo:self.c0 + hi].bitcast(F32)
            return SCF[:, self.c0 + lo:self.c0 + hi]

        def tk(self, lo=0, hi=None):
            hi = self.n if hi is None else hi
            a = (self.c0 + lo) // 512
            b = (self.c0 + hi - 1) // 512
            return (PGR_t if self.sp == "R" else PGF_t)[a:b + 1]

    TMPR = [Buf(0, 512), Buf(1, 512)]
    TMPF = [Buf(0, 512, "F"), Buf(1, 512, "F")]
    R0 = Buf(2, 512, "F")
    R1 = Buf(3, 512, "F")
    R2 = Buf(4, 512, "F")
    tmp_i = [0, 0]

    def ntmpr():
        tmp_i[0] ^= 1
        return TMPR[tmp_i[0]]

    def ntmpf():
        tmp_i[1] ^= 1
        return TMPF[tmp_i[1]]

    ident = con[:, CC_ID:CC_ID + 128]
    onesR = conR[:, CC_ONES:CC_ONES + 128]
    bdR = conR[:, CC_BD:CC_BD + 128]
    oloR = conR[:, CC_OLO:CC_OLO + 128]
    ohiR = conR[:, CC_OHI:CC_OHI + 128]
    protR = conR[:, CC_PROT:CC_PROT + 128]
    mcur = con[:, CC_MCUR:CC_MCUR + 128]
    mprev = con[:, CC_MPREV:CC_MPREV + 128]
    mnew = con[:, CC_MNEW:CC_MNEW + 128]
    mpast = con[:, CC_MPAST:CC_MPAST + 8]

    zcol = con[:, CC_OLO + 64:CC_OLO + 65]
    P.dma("par", lambda e: e.dma_start(out=par[:], in_=par_d), writes=[par_t])
    P.dma("con", lambda e: e.dma_start(out=con[:], in_=con_d), writes=[con_t])
    P.dma("conR", lambda e: [e.dma_start(out=conR[:], in_=con_d), e.dma_start(out=gsel[:], in_=gsel_d)],
          writes=[con_t], n=2, eng="pool")
    P.op("act", lambda e: e.activation(sinkexp[:], par[:, :, PC_SK:PC_SK + 4], AF.Exp), reads=[par_t], writes=[sk_t])
    for l in range(L):
        P.op("pool", (lambda l: lambda e: e.tensor_copy(Atail[:, l].rearrange("p a b -> p (a b)"), zcol.broadcast_to([128, 120])))(l),
             reads=[con_t], writes=[halo_t[l][0]])

    wlist = []
    for (b0, nb) in tiles:
        for l in range(depth):
            off = 0
            for e_ in SCHED:
                wlist.append((l, off, e_[-1]))
                off += e_[-1]
    wstate = {"issued": 0, "next": 0}

    def w_issue(upto):
        while wstate["issued"] < min(upto, len(wlist)):
            i = wstate["issued"]
            l, off, n = wlist[i]
            s = i % NWS
            P.dma("ws%d" % s, (lambda s, l, off, n: lambda e: e.dma_start(out=WS[s][:, 0:n], in_=wl_d[l, :, off:off + n]))(s, l, off, n),
                  writes=[WS_t[s]], eng="pool")
            wstate["issued"] += 1

    def wnext(n_expect):
        i = wstate["next"]
        assert wlist[i][2] == n_expect, (wlist[i], n_expect)
        w_issue(i + NWS)
        wstate["next"] += 1
        s = i % NWS
        return WS[s], WS_t[s]

    def par_c(l, c):
        return par[:, l, c:c + 1]

    def rmsnorm(l, pc, T):
        bs = balloc()
        for k in range(8):
            t = ntmpr()
            P.op("act", (lambda t, k: lambda e: e.activation(t.r(0, T), xT[:, k, 0:T], AF.Square))(t, k),
                 reads=[xT_t[k]], writes=t.tk())
            P.op("pe", (lambda t, k: lambda e: e.matmul(BANK[bs][:, 0:T], onesR, t.r(0, T), start=(k == 0), stop=(k == 7)))(t, k),
                 reads=t.tk() + [con_t], writes=[BANK_t[bs]])
        P.op("act", lambda e: e.activation(R0.f(0, T), BANK[bs][:, 0:T], AF.Ln, bias=EPS, scale=1.0 / D),
             reads=[BANK_t[bs]], writes=R0.tk())
        bfree(bs)
        P.op("act", lambda e: e.activation(R0.f(0, T), R0.f(0, T), AF.Exp, scale=-0.5), reads=R0.tk(), writes=R0.tk())
        for k in range(8):
            P.op("dve", (lambda k: lambda e: e.scalar_tensor_tensor(out=hT[:, k, 0:T], in0=xT[:, k, 0:T], scalar=par_c(l, pc + k),
                                                                      in1=R0.f(0, T), op0=ALU.mult, op1=ALU.mult))(k),
                 reads=[xT_t[k], par_t] + R0.tk(), writes=[hT_t[k]])

    ACT_B = Buf(2, 11 * 512)

    def ffn(l, T):
        for hf in range(2):
            for j in range(11):
                w, wt = wnext(2048)
                wv = w[:, 0:2048].rearrange("p (k c) -> p k c", k=8)
                bg, bu = balloc(), balloc()
                for (bb, c0) in ((bg, 0), (bu, 128)):
                    for k in range(8):
                        P.op("pe", (lambda bb, c0, k, wv: lambda e: e.matmul(BANK[bb][:, 0:T], wv[:, k, c0:c0 + 128], hT[:, k, 0:T],
                                                                            start=(k == 0), stop=(k == 7)))(bb, c0, k, wv),
                             reads=[wt, hT_t[k]], writes=[BANK_t[bb]])
                t = ntmpf()
                P.op("act", (lambda t, bg: lambda e: e.activation(t.f(0, T), BANK[bg][:, 0:T], AF.Silu))(t, bg),
                     reads=[BANK_t[bg]], writes=t.tk())
                P.op("dve", (lambda t, bu, j: lambda e: e.tensor_tensor(out=ACT_B.r(j * 512, j * 512 + T), in0=t.f(0, T), in1=BANK[bu][:, 0:T],
                                                                         op=ALU.mult))(t, bu, j),
                     reads=t.tk() + [BANK_t[bu]], writes=ACT_B.tk(j * 512, j * 512 + T))
                bfree(bg)
                bfree(bu)
            for dp in range(4):
                w, wt = wnext(2816)
                wv = w[:, 0:2816].rearrange("p (k c) -> p k c", k=11)
                for d in range(2):
                    by = balloc()
                    for fi in range(11):
                        P.op("pe", (lambda by, fi, d, wv: lambda e: e.matmul(BANK[by][:, 0:T], wv[:, fi, d * 128:(d + 1) * 128],
                                                                            ACT_B.r(fi * 512, fi * 512 + T), start=(fi == 0), stop=(fi == 10)))(by, fi, d, wv),
                             reads=[wt] + ACT_B.tk(fi * 512, fi * 512 + T), writes=[BANK_t[by]])
                    kk = dp * 2 + d
                    P.op("dve", (lambda by, kk: lambda e: e.scalar_tensor_tensor(out=xT[:, kk, 0:T], in0=BANK[by][:, 0:T], scalar=0.5,
                                                                                  in1=xT[:, kk, 0:T], op0=ALU.mult, op1=ALU.add))(by, kk),
                         reads=[BANK_t[by], xT_t[kk]], writes=[xT_t[kk]])
                    bfree(by)

    AST = 896
    AT = Buf(2, 4 * AST)
    DGC = [Buf(13, 31 * 128), Buf(21, 31 * 128)]
    CB = Buf(9, 4 * 512)
    VGE = [Buf(2, 512), Buf(3, 512), Buf(6, 512), Buf(7, 512)]
    VGO = [Buf(4, 512), Buf(5, 512), Buf(8, 512), Buf(18, 512)]
    WSM = Buf(13, 1024)
    WSSM = Buf(15, 1024)
    BSP = Buf(17, 256)
    QT = Buf(2, 4 * 512)
    QN = [Buf(6, 512), Buf(11, 512, "F")]
    KT = Buf(7, 1024)
    VZ = Buf(9, 5 * 256)
    PT = [Buf(12, 512), Buf(13, 512), Buf(14, 512), Buf(15, 512)]
    KCT = Buf(16, 4 * 512)
    VZC = Buf(20, 4 * 512)
    ON = Buf(24, 4 * 512)
    UB = Buf(28, 4 * 512)
    SA = Buf(32, 4 * 512)
    MIX = Buf(2, 8 * 512)
    STG = [Buf(5, 1024, "F"), Buf(7, 1024, "F")]
    STC = Buf(5, 4 * 512, "F")
    LNP = Buf(5, 4 * 512, "F")
    KCR = Buf(5, 4 * 512, "F")
    VT = [Buf(9, 512, "F"), Buf(10, 512, "F"), Buf(13, 512, "F"), Buf(14, 512, "F")]
    COS = Buf(9, 512, "F")
    SIN = Buf(10, 512, "F")
    WSR = Buf(11, 1024, "F")
    WSSR = Buf(13, 1024, "F")

    STOP = os.environ.get("MK_STOP", "")

    class StopBuild(Exception):
        pass

    def stopchk(stage):
        if STOP and STOP == stage:
            raise StopBuild()

    def mixer(l, ti, b0, nb, T, Tp, has_s):
        tile0 = (ti == 0)

        def build_diag(c, j0, j1):
            dg = DGC[c % 2]
            for j in range(j0, j1):
                P.op("dve", (lambda dg, j, c: lambda e: e.tensor_scalar(out=dg.r(j * 128, (j + 1) * 128), in0=ident,
                                                                          scalar1=par_c(l, PC_WDW + c * 31 + j), scalar2=None, op0=ALU.mult))(dg, j, c),
                     reads=[con_t, par_t], writes=dg.tk(j * 128, (j + 1) * 128))

        P.op("pool", lambda e: e.tensor_copy(AT.r().rearrange("p (c x) -> p c x", c=4)[:, :, 0:30], Atail[:, l]),
             reads=[halo_t[l][0]], writes=AT.tk())
        if has_s:
            stg = STC
            P.dma("stc", lambda e: e.dma_start(out=stg.f().rearrange("p (g c) -> p g c", g=4)[0:120],
                                               in_=stc_d[l].rearrange("(g s) r c -> (s r) g c", g=4)), writes=stg.tk())
            P.dma("convs_cp", lambda e: [e.dma_start(out=convs_d[l, g * 4 + s_, 0:22, :],
                                                     in_=stg.f().rearrange("p (g c) -> p g c", g=4)[s_ * 30 + 8:s_ * 30 + 30, g, :])
                                         for g in range(4) for s_ in range(4)], reads=stg.tk(), n=16)
            for c in range(4):
                bt = balloc()
                for g in range(4):
                    P.op("pe", (lambda bt, g, c: lambda e: e.transpose(BANK[bt][:, g * 120:(g + 1) * 120],
                                                                         stg.f().rearrange("p (g c) -> p g c", g=4)[0:120, g, c * 128:(c + 1) * 128],
                                                                         ident[0:120, 0:120]))(bt, g, c),
                         reads=stg.tk() + [con_t], writes=[BANK_t[bt]])
                P.op("act", (lambda bt, c: lambda e: e.copy(
                    AT.r(c * AST + 30 + Tp, c * AST + 30 + Tp + 608).rearrange("p (s x) -> p s x", s=16)[:, :, 0:30],
                    BANK[bt][:, 0:480].rearrange("p (s x) -> p s x", s=16)))(bt, c),
                    reads=[BANK_t[bt]], writes=AT.tk(c * AST, (c + 1) * AST))
                bfree(bt)
        stopchk("s1_stc_done")
        for c in range(4):
            build_diag(c // 2, (c % 2) * 16, min(31, (c % 2) * 16 + 16))
            w, wt = wnext(2048)
            wv = w[:, 0:2048].rearrange("p (k c) -> p k c", k=8)
            ba, bg = balloc(), balloc()
            for (bb, c0) in ((ba, 0), (bg, 128)):
                for k in range(8):
                    P.op("pe", (lambda bb, c0, k, wv: lambda e: e.matmul(BANK[bb][:, 0:T], wv[:, k, c0:c0 + 128], hT[:, k, 0:T],
                                                                        start=(k == 0), stop=(k == 7)))(bb, c0, k, wv),
                         reads=[wt, hT_t[k]], writes=[BANK_t[bb]])
            t = ntmpf()
            P.op("act", (lambda t, bg: lambda e: e.activation(t.f(0, T), BANK[bg][:, 0:T], AF.Sigmoid))(t, bg),
                 reads=[BANK_t[bg]], writes=t.tk())
            P.op("dve", (lambda t, ba, c: lambda e: e.tensor_tensor(out=AT.r(c * AST + 30, c * AST + 30 + Tp), in0=t.f(0, Tp),
                                                                     in1=BANK[ba][:, 0:Tp], op=ALU.mult))(t, ba, c),
                 reads=t.tk() + [BANK_t[ba]], writes=AT.tk(c * AST, (c + 1) * AST))
            if has_s:
                P.op("dve", (lambda t, ba, c: lambda e: e.tensor_tensor(
                    out=AT.r(c * AST + 30 + Tp, c * AST + 30 + Tp + 608).rearrange("p (s x) -> p s x", s=16)[:, :, 30:38],
                    in0=t.f(Tp, T).rearrange("p (s x) -> p s x", s=16),
                    in1=BANK[ba][:, Tp:T].rearrange("p (s x) -> p s x", s=16), op=ALU.mult))(t, ba, c),
                    reads=t.tk() + [BANK_t[ba]], writes=AT.tk(c * AST, (c + 1) * AST))
            bfree(ba)
            bfree(bg)
        stopchk("s2_glu_done")
        if not has_s:
            P.op("pool", lambda e: e.tensor_copy(Atail[:, l], AT.r().rearrange("p (c x) -> p c x", c=4)[:, :, Tp:Tp + 30]),
                 reads=AT.tk(), writes=[halo_t[l][0]])
        else:
            bt = balloc()
            for c in range(4):
                P.op("pe", (lambda c, bt: lambda e: e.transpose(BANK[bt][:, c * 128:(c + 1) * 128],
                                                             AT.f(c * AST + Tp - 98, c * AST + Tp + 30), ident))(c, bt),
                     reads=AT.tk(c * AST, (c + 1) * AST) + [con_t], writes=[BANK_t[bt]])
            st0 = ntmpf()
            P.op("act", (lambda bt: lambda e: e.copy(st0.f(), BANK[bt][:, :]))(bt), reads=[BANK_t[bt]], writes=st0.tk())
            bfree(bt)
            P.dma("convp", lambda e: e.dma_start(out=convp_d[l], in_=st0.f()[98:128]), reads=st0.tk())
            bt = balloc()
            for c in range(4):
                tc_ = ntmpf()
                P.op("dve", (lambda c, tc_: lambda e: e.tensor_copy(
                    tc_.f(0, 128).rearrange("p (s x) -> p s x", s=16),
                    AT.f(c * AST + 30 + Tp, c * AST + 30 + Tp + 608).rearrange("p (s x) -> p s x", s=16)[:, :, 30:38]))(c, tc_),
                    reads=AT.tk(c * AST, (c + 1) * AST), writes=tc_.tk())
                P.op("pe", (lambda c, tc_, bt: lambda e: e.transpose(BANK[bt][:, c * 128:(c + 1) * 128], tc_.f(0, 128), ident))(c, tc_, bt),
                     reads=tc_.tk() + [con_t], writes=[BANK_t[bt]])
            st1 = ntmpf()
            P.op("act", (lambda bt: lambda e: e.copy(st1.f(), BANK[bt][:, :]))(bt), reads=[BANK_t[bt]], writes=st1.tk())
            bfree(bt)
            P.dma("convs_new", lambda e: [e.dma_start(out=convs_d[l, s, 22:30, :], in_=st1.f()[s * 8:(s + 1) * 8]) for s in range(16)],
                  reads=st1.tk(), n=16)
        stopchk("s3_convout_done")
        bs1, bs2 = balloc(), balloc()
        def conv_chunk(c):
            bc = balloc()
            dg = DGC[c % 2]
            for j in range(31):
                P.op("pe", (lambda dg, j, c, bc: lambda e: e.matmul(BANK[bc][:, 0:Tp], dg.r(j * 128, (j + 1) * 128),
                                                                   AT.r(c * AST + j, c * AST + j + Tp), start=(j == 0), stop=(j == 30)))(dg, j, c, bc),
                     reads=dg.tk(j * 128, (j + 1) * 128) + AT.tk(c * AST, (c + 1) * AST), writes=[BANK_t[bc]])
            if has_s:
                bsa, bsb = balloc(), balloc()
                a0 = c * AST + 30 + Tp
                for j in range(31):
                    P.op("pe", (lambda dg, j, a0, bsa: lambda e: e.matmul(BANK[bsa][:, 0:464], dg.r(j * 128, (j + 1) * 128),
                                                                       AT.r(a0 + j, a0 + j + 464), start=(j == 0), stop=(j == 30)))(dg, j, a0, bsa),
                         reads=dg.tk(j * 128, (j + 1) * 128) + AT.tk(c * AST, (c + 1) * AST), writes=[BANK_t[bsa]])
                    P.op("pe", (lambda dg, j, a0, bsb: lambda e: e.matmul(BANK[bsb][:, 0:84], dg.r(j * 128, (j + 1) * 128),
                                                                       AT.r(a0 + 494 + j, a0 + 494 + j + 84), start=(j == 0), stop=(j == 30)))(dg, j, a0, bsb),
                         reads=dg.tk(j * 128, (j + 1) * 128) + AT.tk(c * AST, (c + 1) * AST), writes=[BANK_t[bsb]])
            if c + 2 < 4:
                build_diag(c + 2, 0, 31)
            t = ntmpr()
            segs = [(BANK[bc][:, 0:Tp], 0, Tp, bc)]
            if has_s:
                segs.append((BANK[bsa][:, 0:494].rearrange("p (s x) -> p s x", x=38)[:, :, 0:8], Tp, Tp + 104, bsa))
                segs.append((BANK[bsb][:, 0:114].rearrange("p (s x) -> p s x", x=38)[:, :, 0:8], Tp + 104, T, bsb))
            for (src, lo, hi, bsrc) in segs:
                def shp(ap, src=src):
                    return ap.rearrange("p (s x) -> p s x", x=8) if len(src.shape) == 3 else ap
                P.op("act", (lambda c, src, lo, hi, shp: lambda e: e.activation(shp(CB.r(c * 512 + lo, c * 512 + hi)), src, AF.Identity,
                                                                              bias=par_c(l, PC_BC + c)))(c, src, lo, hi, shp),
                     reads=[BANK_t[bsrc], par_t], writes=CB.tk(c * 512, c * 512 + T))
                P.op("act", (lambda c, src, lo, hi, shp, t: lambda e: e.activation(shp(t.r(lo, hi)), src, AF.Square, bias=par_c(l, PC_BC + c)))(c, src, lo, hi, shp, t),
                     reads=[BANK_t[bsrc], par_t], writes=t.tk())
            bfree(bc)
            if has_s:
                bfree(bsa)
                bfree(bsb)
            P.op("pe", (lambda c: lambda e: e.matmul(BANK[bs1][:, 0:T], onesR, CB.r(c * 512, c * 512 + T), start=(c == 0), stop=(c == 3)))(c),
                 reads=CB.tk(c * 512, c * 512 + T) + [con_t], writes=[BANK_t[bs1]])
            P.op("pe", (lambda c, t: lambda e: e.matmul(BANK[bs2][:, 0:T], onesR, t.r(0, T), start=(c == 0), stop=(c == 3)))(c, t),
                 reads=t.tk() + [con_t], writes=[BANK_t[bs2]])
        for c in range(4):
            conv_chunk(c)
        w, wt = wnext(4096)
        wv = w[:, 0:4096].rearrange("p (k c) -> p k c", k=8)
        for cu in range(4):
            bu = balloc()
            for k in range(8):
                P.op("pe", (lambda bu, k, cu, wv: lambda e: e.matmul(BANK[bu][:, 0:T], wv[:, k, cu * 128:(cu + 1) * 128], hT[:, k, 0:T],
                                                                    start=(k == 0), stop=(k == 7)))(bu, k, cu, wv),
                     reads=[wt, hT_t[k]], writes=[BANK_t[bu]])
            P.op("act", (lambda bu, cu: lambda e: e.activation(UB.r(cu * 512, cu * 512 + T), BANK[bu][:, 0:T], AF.Gelu_apprx_tanh))(bu, cu),
                 reads=[BANK_t[bu]], writes=UB.tk(cu * 512, cu * 512 + T))
            bfree(bu)
        def ln_a():
            t0 = ntmpf()
            P.op("act", lambda e: e.mul(R1.f(0, T), BANK[bs1][:, 0:T], 1.0 / 512), reads=[BANK_t[bs1]], writes=R1.tk())
            P.op("dve", lambda e: e.tensor_tensor(out=t0.f(0, T), in0=R1.f(0, T), in1=R1.f(0, T), op=ALU.mult), reads=R1.tk(), writes=t0.tk())
            P.op("dve", lambda e: e.scalar_tensor_tensor(out=R0.f(0, T), in0=BANK[bs2][:, 0:T], scalar=1.0 / 512, in1=t0.f(0, T),
                                                         op0=ALU.mult, op1=ALU.subtract), reads=[BANK_t[bs2]] + t0.tk(), writes=R0.tk())
            bfree(bs1)
            bfree(bs2)
            P.op("act", lambda e: e.activation(R0.f(0, T), R0.f(0, T), AF.Ln, bias=EPS), reads=R0.tk(), writes=R0.tk())
            P.op("act", lambda e: e.activation(R0.f(0, T), R0.f(0, T), AF.Exp, scale=-0.5), reads=R0.tk(), writes=R0.tk())
            t1 = R2
            P.op("dve", lambda e: e.tensor_tensor(out=t1.f(0, T), in0=R1.f(0, T), in1=R0.f(0, T), op=ALU.mult), reads=R0.tk() + R1.tk(), writes=t1.tk())

        def ln_b(c0_, c1_):
            t1 = R2
            for c in range(c0_, c1_):
                tq = ntmpf()
                P.op("dve", (lambda c, tq: lambda e: e.tensor_tensor(out=tq.f(0, T), in0=CB.f(c * 512, c * 512 + T), in1=R0.f(0, T),
                                                                  op=ALU.mult))(c, tq), reads=CB.tk(c * 512, c * 512 + T) + R0.tk(), writes=tq.tk())
                P.op("dve", (lambda c, tq: lambda e: e.tensor_tensor(out=tq.f(0, T), in0=tq.f(0, T), in1=t1.f(0, T),
                                                                  op=ALU.subtract))(c, tq), reads=tq.tk() + t1.tk(), writes=tq.tk())
                P.op("act", (lambda c, tq: lambda e: e.activation(SA.r(c * 512, c * 512 + T), tq.f(0, T), AF.Silu,
                                                               bias=par_c(l, PC_LB + c), scale=par_c(l, PC_LG + c)))(c, tq),
                     reads=tq.tk() + [par_t], writes=SA.tk(c * 512, c * 512 + T))

        stopchk("s4_conv_done")
        P.dma("lnp", lambda e: e.dma_start(out=LNP.f().rearrange("p (a c) -> p a c", a=4), in_=lnp_d[l:l + 1].broadcast_to([128, 4, 512])),
              writes=LNP.tk())
        P.dma("wsr", lambda e: e.dma_start(out=WSR.f().rearrange("p (g t) -> p g t", g=8), in_=wsT_d[l]), writes=WSR.tk())
        P.op("pool", lambda e: e.tensor_tensor(out=WSM.r().rearrange("p (g t) -> p g t", g=8), in0=WSR.f().rearrange("p (g t) -> p g t", g=8),
                                               in1=mcur.unsqueeze(1).broadcast_to([128, 8, 128]), op=ALU.mult),
             reads=WSR.tk() + [con_t], writes=WSM.tk())
        P.dma("bsp", lambda e: e.dma_start(out=BSP.r(), in_=bsp_d[l]), writes=BSP.tk(), eng="pool")
        if has_s:
            P.dma("wssr", lambda e: e.dma_start(out=WSSR.f().rearrange("p (g t) -> p g t", g=8), in_=wsS_d[l]), writes=WSSR.tk())
            P.op("pool", lambda e: e.tensor_tensor(out=WSSM.r().rearrange("p (g t) -> p g t", g=8), in0=WSSR.f().rearrange("p (g t) -> p g t", g=8),
                                                   in1=mcur.unsqueeze(1).broadcast_to([128, 8, 128]), op=ALU.mult),
                 reads=WSSR.tk() + [con_t], writes=WSSM.tk())
        stopchk("g0")
        w, wt = wnext(4096)
        wv = w[:, 0:4096].rearrange("p (k c) -> p k c", k=8)
        stopchk("g1")
        nvt = 2 if has_s else 4
        bsg = [balloc() for _ in range(4)]
        lnv = LNP.f().rearrange("p (a c) -> p a c", a=4)
        def v_part(b):
            is_s = has_s and b == nb - 1
            bv = balloc()
            for k in range(8):
                P.op("pe", (lambda bv, k, b, wv: lambda e: e.matmul(BANK[bv][:, :], hT[:, k, b * 128:(b + 1) * 128], wv[:, k, :],
                                                                   start=(k == 0), stop=(k == 7)))(bv, k, b, wv),
                     reads=[wt, hT_t[k]], writes=[BANK_t[bv]])
            vt, vge, vgo = VT[b % nvt], VGE[b], VGO[b]
            sti = b % 2
            sv = stat[:, sti]
            P.op("act", (lambda bv, vt: lambda e: e.activation(vt.f(), BANK[bv][:, :], AF.Gelu_apprx_tanh))(bv, vt),
                 reads=[BANK_t[bv]], writes=vt.tk())
            bfree(bv)
            P.op("dve", (lambda vt, sv: lambda e: e.bn_stats(sv[:, 0:6], vt.f()))(vt, sv), reads=vt.tk(), writes=[stat_t[sti]])
            P.op("dve", (lambda sv: lambda e: e.bn_aggr(sv[:, 6:8], sv[:, 0:6]))(sv), reads=[stat_t[sti]], writes=[stat_t[sti]])
            P.op("act", (lambda sv: lambda e: e.activation(sv[:, 8:9], sv[:, 7:8], AF.Ln, bias=EPS))(sv), reads=[stat_t[sti]], writes=[stat_t[sti]])
            P.op("act", (lambda sv: lambda e: e.activation(sv[:, 9:10], sv[:, 8:9], AF.Exp, scale=-0.5))(sv), reads=[stat_t[sti]], writes=[stat_t[sti]])
            P.op("dve", (lambda vt, sv: lambda e: e.tensor_scalar(out=vt.f(), in0=vt.f(), scalar1=sv[:, 6:7], scalar2=sv[:, 9:10],
                                                                   op0=ALU.subtract, op1=ALU.mult))(vt, sv),
                 reads=vt.tk() + [stat_t[sti]], writes=vt.tk())
            for (dst, ig, ib) in ((vge, 0, 1), (vgo, 2, 3)):
                P.op("dve", (lambda dst, ig, vt: lambda e: e.tensor_tensor(out=dst.r(), in0=vt.f(), in1=lnv[:, ig], op=ALU.mult))(dst, ig, vt),
                     reads=vt.tk() + LNP.tk(), writes=dst.tk())
                P.op("dve", (lambda dst, ib: lambda e: e.tensor_tensor(out=dst.r(), in0=dst.f(), in1=lnv[:, ib], op=ALU.add))(dst, ib),
                     reads=LNP.tk() + dst.tk(), writes=dst.tk())
            if is_s:
                P.op("pool", (lambda vt, vge, vgo: lambda e: e.tensor_tensor(out=vt.f(), in0=vge.f(), in1=vgo.f(), op=ALU.add))(vt, vge, vgo),
                     reads=vge.tk() + vgo.tk(), writes=vt.tk())
                P.dma("gv", (lambda vt: lambda e: e.dma_start(out=gv_d[l], in_=vt.f()))(vt), reads=vt.tk())

        def s_part(b):
            is_s = has_s and b == nb - 1
            vge, vgo = VGE[b], VGO[b]
            wsx = (WSSM if (is_s and not os.environ.get("MK_T1")) else WSM).r().rearrange("p (g t) -> p g t", g=8)
            bo = 128 if (is_s and not os.environ.get("MK_T2")) else 0
            for cp in range(4):
                o_ap = BANK[bsg[cp]][:, b * 128:(b + 1) * 128]
                P.op("pe", (lambda o_ap, vge, cp, wsx: lambda e: e.matmul(o_ap, vge.r(cp * 128, (cp + 1) * 128), wsx[:, 2 * cp], start=True, stop=False))(o_ap, vge, cp, wsx),
                     reads=vge.tk() + WSM.tk() + WSSM.tk(), writes=[BANK_t[bsg[cp]]])
                P.op("pe", (lambda o_ap, vgo, cp, wsx: lambda e: e.matmul(o_ap, vgo.r(cp * 128, (cp + 1) * 128), wsx[:, 2 * cp + 1], start=False, stop=False))(o_ap, vgo, cp, wsx),
                     reads=vgo.tk() + WSM.tk() + WSSM.tk(), writes=[BANK_t[bsg[cp]]])
                P.op("pe", (lambda o_ap, cp, bo: lambda e: e.matmul(o_ap, gsel[:, cp, :], BSP.r(bo, bo + 128), start=False, stop=True))(o_ap, cp, bo),
                     reads=BSP.tk() + [con_t], writes=[BANK_t[bsg[cp]]])

        for b in range(nb):
            v_part(b)
        ln_a()
        ln_b(0, 4)
        for b in range(nb):
            s_part(b)
        stopchk("g2")
        for cp in range(4):
            P.op("dve", (lambda cp: lambda e: e.tensor_tensor(out=UB.r(cp * 512, cp * 512 + T), in0=UB.f(cp * 512, cp * 512 + T),
                                                               in1=BANK[bsg[cp]][:, 0:T], op=ALU.mult))(cp),
                 reads=UB.tk(cp * 512, cp * 512 + T) + [BANK_t[bsg[cp]]], writes=UB.tk(cp * 512, cp * 512 + T))
            bfree(bsg[cp])

        stopchk("s5_gmlp_done")
        tok0 = b0 * 128
        P.dma("cos", lambda e: e.dma_start(out=COS.f(0, T), in_=cos_d[:, tok0:tok0 + T]), writes=COS.tk())
        P.dma("sin", lambda e: e.dma_start(out=SIN.f(0, T), in_=sin_d[:, tok0:tok0 + T]), writes=SIN.tk())
        P.op("pool", lambda e: e.tensor_copy(VZ.r(), zcol.broadcast_to([128, 1280])), reads=[con_t], writes=VZ.tk())
        if not tile0:
            P.op("pool", lambda e: e.tensor_copy(VZ.r(0, 256), VZhalo[:, l].rearrange("p a b -> p (a b)")), reads=[halo_t[l][2]], writes=VZ.tk())
            P.op("pool", lambda e: e.tensor_copy(KT.r(0, 128), Khalo[:, l]), reads=[halo_t[l][1]], writes=KT.tk())

        stopchk("a0")

        QNB = [QN[0], Buf(12, 512)]
        kvw = {}
        st_ = {}

        def qk_A(i):
            if i == 0:
                w_, wt_ = wnext(4096)
                kvw["q"] = (w_[:, 0:4096].rearrange("p (k c) -> p k c", k=8), wt_)
            if i == 4:
                w_, wt_ = wnext(2048)
                kvw["kv"] = (w_[:, 0:2048].rearrange("p (k c) -> p k c", k=8), wt_)
            wv_, wt_ = kvw["q"] if i < 4 else kvw["kv"]
            c0_ = i * 128 if i < 4 else 0
            bq = balloc()
            for k in range(8):
                P.op("pe", (lambda bq, k, c0_, wv_: lambda e: e.matmul(BANK[bq][:, 0:T], wv_[:, k, c0_:c0_ + 128], hT[:, k, 0:T],
                                                                      start=(k == 0), stop=(k == 7)))(bq, k, c0_, wv_),
                     reads=[wt_, hT_t[k]], writes=[BANK_t[bq]])
            st_[i] = {"bq": bq}

        def qk_B(i):
            d = st_[i]
            bq = d["bq"]
            gcol = PC_QG if i < 4 else PC_KG
            t = ntmpr()
            qn = QNB[i % 2]
            P.op("act", (lambda t, bq: lambda e: e.activation(t.r(0, T), BANK[bq][:, 0:T], AF.Square))(t, bq), reads=[BANK_t[bq]], writes=t.tk())
            P.op("act", (lambda qn, bq, gcol: lambda e: e.activation(qn.r(0, T), BANK[bq][:, 0:T], AF.Copy, scale=par_c(l, gcol)))(qn, bq, gcol),
                 reads=[BANK_t[bq], par_t], writes=qn.tk())
            bfree(bq)
            bss = balloc()
            P.op("pe", (lambda bss, t: lambda e: e.matmul(BANK[bss][:, 0:T], bdR, t.r(0, T), start=True, stop=True))(bss, t), reads=t.tk() + [con_t], writes=[BANK_t[bss]])
            br = balloc()
            P.op("pe", (lambda br, qn: lambda e: e.matmul(BANK[br][:, 0:T], protR, qn.r(0, T), start=True, stop=True))(br, qn), reads=qn.tk() + [con_t], writes=[BANK_t[br]])
            d.update(bss=bss, br=br, qn=qn)

        def qk_C(i):
            d = st_[i]
            bss, br, qn = d["bss"], d["br"], d["qn"]
            if i < 4:
                npb_ = nb - 1 if has_s else nb
                dsts = [(QT.r().rearrange("p (b g t) -> p b g t", g=4, t=128)[:, 0:npb_, i, :], 0, npb_ * 128, 128)]
                if has_s:
                    dsts.append((QT.r((nb - 1) * 512, nb * 512).rearrange("p (s g q) -> p s g q", g=4, q=8)[:, :, i, :], Tp, T, 8))
                dst_tk = QT.tk()
            else:
                dsts = [(KT.r(128, 128 + T).rearrange("p (a b) -> p a b", b=128), 0, T, 128)]
                dst_tk = KT.tk()
            P.op("act", (lambda bss: lambda e: e.activation(R0.f(0, T), BANK[bss][:, 0:T], AF.Ln, bias=EPS, scale=1.0 / 64))(bss), reads=[BANK_t[bss]], writes=R0.tk())
            bfree(bss)
            P.op("act", lambda e: e.activation(R0.f(0, T), R0.f(0, T), AF.Exp, scale=-0.5), reads=R0.tk(), writes=R0.tk())
            t2 = QN[1]
            P.op("dve", (lambda qn: lambda e: e.tensor_tensor(out=t2.f(0, T), in0=qn.f(0, T), in1=COS.f(0, T), op=ALU.mult))(qn), reads=qn.tk() + COS.tk(), writes=t2.tk())
            t3 = ntmpf()
            P.op("dve", (lambda br, t3: lambda e: e.tensor_tensor(out=t3.f(0, T), in0=BANK[br][:, 0:T], in1=SIN.f(0, T), op=ALU.mult))(br, t3),
                 reads=[BANK_t[br]] + SIN.tk(), writes=t3.tk())
            bfree(br)
            P.op("dve", (lambda t3: lambda e: e.tensor_tensor(out=t2.f(0, T), in0=t2.f(0, T), in1=t3.f(0, T), op=ALU.add))(t3), reads=t2.tk() + t3.tk(), writes=t2.tk())
            for (dst_ap, lo, hi, shp) in dsts:
                P.op("dve", (lambda dst_ap, lo, hi, shp: lambda e: e.tensor_tensor(
                    out=dst_ap, in0=t2.f(lo, hi).rearrange("p (a b) -> p a b", b=shp), in1=R0.f(lo, hi).rearrange("p (a b) -> p a b", b=shp),
                    op=ALU.mult))(dst_ap, lo, hi, shp), reads=t2.tk() + R0.tk(), writes=dst_tk)

        qk_A(0)
        qk_A(1)
        qk_A(2)
        qk_B(0)
        for i in range(5):
            if i + 3 < 5:
                qk_A(i + 3)
            if i + 1 < 5:
                qk_B(i + 1)
            qk_C(i)
        wv, wt = kvw["kv"]
        stopchk("a2")
        vzv = VZ.r().rearrange("p (b a d) -> p b a d", b=5, a=2)
        for b in range(nb):
            is_s = has_s and b == nb - 1
            bv = balloc()
            for k in range(8):
                P.op("pe", (lambda bv, k, b, wv: lambda e: e.matmul(BANK[bv][:, 0:128], hT[:, k, b * 128:(b + 1) * 128], wv[:, k, 128:256],
                                                                   start=(k == 0), stop=(k == 7)))(bv, k, b, wv),
                     reads=[wt, hT_t[k]], writes=[BANK_t[bv]])
            P.op("act", (lambda bv, b: lambda e: e.copy(vzv[:, 1 + b, 0, 0:64], BANK[bv][:, 0:64]))(bv, b), reads=[BANK_t[bv]], writes=VZ.tk())
            P.op("act", (lambda bv, b: lambda e: e.copy(vzv[:, 1 + b, 1, 64:128], BANK[bv][:, 64:128]))(bv, b), reads=[BANK_t[bv]], writes=VZ.tk())
            stopchk("av%d" % b)
            is_lastp = has_s and b == nb - 2
            if (is_s or is_lastp) and not os.environ.get("MK_T3"):
                vo_d = (vs_d if is_s else vp_d)
                P.dma("vout", (lambda b, vo_d: lambda e: [e.dma_start(out=vo_d[l][:, kv * 64:(kv + 1) * 64],
                                                                      in_=vzv[:, 1 + b, kv, kv * 64:(kv + 1) * 64].bitcast(F32)) for kv in range(2)])(b, vo_d),
                      reads=VZ.tk(), n=2)
                bt = balloc()
                P.op("pe", (lambda bt, b: lambda e: e.matmul(BANK[bt][:, 0:128], KT.r(128 + b * 128, 256 + b * 128), conR[:, CC_ID:CC_ID + 128],
                                                            start=True, stop=True))(bt, b),
                     reads=KT.tk() + [con_t], writes=[BANK_t[bt]])
                st2 = ntmpf()
                P.op("act", (lambda bt, st2: lambda e: e.copy(st2.f(0, 128), BANK[bt][:, 0:128]))(bt, st2), reads=[BANK_t[bt]], writes=st2.tk())
                bfree(bt)
                P.dma("kout", (lambda st2, is_s: lambda e: e.dma_start(out=(ks_d if is_s else kp_d)[l], in_=st2.f(0, 128)))(st2, is_s), reads=st2.tk())
            bfree(bv)
        stopchk("s6_attnproj_done")
        if not has_s:
            P.op("pool", lambda e: e.tensor_copy(Khalo[:, l], KT.r(nb * 128, (nb + 1) * 128)), reads=KT.tk(), writes=[halo_t[l][1]])
            P.op("pool", lambda e: e.tensor_copy(VZhalo[:, l].rearrange("p a b -> p (a b)"), VZ.r(nb * 256, (nb + 1) * 256)), reads=VZ.tk(), writes=[halo_t[l][2]])

        qv = QT.r().rearrange("p (g t) -> p g t", g=4)
        onv = ON.r().rearrange("p (g t) -> p g t", g=4)
        sk_b = sinkexp[:, l].unsqueeze(2).broadcast_to([128, 4, 128])

        def finish(bo_, bd_, col0, samp=False):
            t = ntmpf()
            if samp:
                P.op("dve", lambda e: e.tensor_tensor(out=t.f().rearrange("p (s g q) -> p s g q", g=4, q=8),
                                                      in0=BANK[bd_][:, :].rearrange("p (s g q) -> p s g q", g=4, q=8),
                                                      in1=sinkexp[:, l].unsqueeze(1).unsqueeze(3).broadcast_to([128, 16, 4, 8]), op=ALU.add),
                     reads=[BANK_t[bd_], sk_t], writes=t.tk())
            else:
                P.op("dve", lambda e: e.tensor_tensor(out=t.f().rearrange("p (g t) -> p g t", g=4), in0=BANK[bd_][:, :].rearrange("p (g t) -> p g t", g=4),
                                                      in1=sk_b, op=ALU.add), reads=[BANK_t[bd_], sk_t], writes=t.tk())
            bfree(bd_)
            P.op("act", lambda e: e.activation(t.f(), t.f(), AF.Ln), reads=t.tk(), writes=t.tk())
            P.op("act", lambda e: e.activation(t.f(), t.f(), AF.Exp, scale=-1.0), reads=t.tk(), writes=t.tk())
            if samp:
                P.op("dve", lambda e: e.tensor_tensor(out=onv[:, :, col0:col0 + 128].rearrange("p g (s q) -> p g s q", q=8),
                                                      in0=BANK[bo_][:, :].rearrange("p (s g q) -> p g s q", g=4, q=8),
                                                      in1=t.f().rearrange("p (s g q) -> p g s q", g=4, q=8), op=ALU.mult),
                     reads=[BANK_t[bo_]] + t.tk(), writes=ON.tk())
            else:
                P.op("dve", lambda e: e.tensor_tensor(out=onv[:, :, col0:col0 + 128], in0=BANK[bo_][:, :].rearrange("p (g t) -> p g t", g=4),
                                                      in1=t.f().rearrange("p (g t) -> p g t", g=4), op=ALU.mult),
                     reads=[BANK_t[bo_]] + t.tk(), writes=ON.tk())
            bfree(bo_)

        npb = nb - 1 if has_s else nb
        PTB = [Buf(16, 512), Buf(17, 512), Buf(18, 512), Buf(19, 512)]
        PTs = [PT, PT if has_s else PTB]

        def attn_s(b):
            has_prev = not (tile0 and b == 0)
            kbs = ([b] if has_prev else []) + [b + 1]
            plist = []
            for kv in range(2):
                for kb in kbs:
                    bs_ = balloc()
                    pt = PTs[b % 2][len(plist)]
                    P.op("pe", (lambda bs_, kv, kb, b: lambda e: e.matmul(BANK[bs_][:, :],
                                                                         KT.r(kb * 128, (kb + 1) * 128)[kv * 64:(kv + 1) * 64],
                                                                         QT.r(b * 512, (b + 1) * 512)[kv * 64:(kv + 1) * 64], start=True, stop=True))(bs_, kv, kb, b),
                         reads=KT.tk() + QT.tk(), writes=[BANK_t[bs_]])
                    P.op("act", (lambda bs_, pt: lambda e: e.activation(pt.r(), BANK[bs_][:, :], AF.Exp, scale=0.125))(bs_, pt),
                         reads=[BANK_t[bs_]], writes=pt.tk())
                    bfree(bs_)
                    mk = mcur if kb == b + 1 else mprev
                    P.op("dve", (lambda pt, mk: lambda e: e.tensor_tensor(out=pt.r().rearrange("p (g t) -> p g t", g=4),
                                                                            in0=pt.f().rearrange("p (g t) -> p g t", g=4),
                                                                            in1=mk.unsqueeze(1).broadcast_to([128, 4, 128]), op=ALU.mult))(pt, mk),
                         reads=pt.tk() + [con_t], writes=pt.tk())
                    plist.append((pt, kv, kb))
            return plist

        def attn_pv(b, plist):
            bo_, bd_ = balloc(), balloc()
            npl = len(plist)
            for i, (pt, kv, kb) in enumerate(plist):
                P.op("pe", (lambda pt, kv, kb, i, bo_, npl: lambda e: e.matmul(BANK[bo_][:, :], vzv[:, kb, kv, :], pt.r(), start=(i == 0), stop=(i == npl - 1)))(pt, kv, kb, i, bo_, npl),
                     reads=pt.tk() + VZ.tk(), writes=[BANK_t[bo_]])
                P.op("pe", (lambda pt, kv, i, bd_, npl: lambda e: e.matmul(BANK[bd_][:, :], oloR if kv == 0 else ohiR, pt.r(), start=(i == 0), stop=(i == npl - 1)))(pt, kv, i, bd_, npl),
                     reads=pt.tk() + [con_t], writes=[BANK_t[bd_]])
            finish(bo_, bd_, b * 128)

        if has_s:
            for b in range(npb):
                attn_pv(b, attn_s(b))
        else:
            pls = {0: attn_s(0)}
            for b in range(1, npb):
                pls[b] = attn_s(b)
                attn_pv(b - 1, pls[b - 1])
            attn_pv(npb - 1, pls[npb - 1])
        if has_s:
            sb = nb - 1
            c0 = sb * 128
            stopchk("s7_attnprompt_done")
            P.dma("kcr", lambda e: [e.dma_start(out=KCR.f(q * 512, (q + 1) * 512).rearrange("p (s d) -> p s d", s=4),
                                                in_=ck_d[l, q * 4:(q + 1) * 4].rearrange("s k d -> k s d")) for q in range(4)], writes=KCR.tk(), n=4)
            for q4 in range(4):
                bt = balloc()
                for s in range(4):
                    sq = q4 * 4 + s
                    P.op("pe", (lambda bt, s, sq: lambda e: e.transpose(BANK[bt][:, s * 128:(s + 1) * 128], KCR.f(sq * 128, (sq + 1) * 128), ident))(bt, s, sq),
                         reads=KCR.tk() + [con_t], writes=[BANK_t[bt]])
                P.op("act", (lambda bt, q4: lambda e: e.copy(KCT.r(q4 * 512, (q4 + 1) * 512), BANK[bt][:, :]))(bt, q4), reads=[BANK_t[bt]], writes=KCT.tk(q4 * 512, (q4 + 1) * 512))
                bfree(bt)
            stopchk("s7b_kctrans_done")
            bsp_ = [balloc(), balloc()]
            for kv in range(2):
                for s in range(16):
                    o_ap = BANK[bsp_[kv]][:, s * 32:(s + 1) * 32]
                    P.op("pe", (lambda o_ap, kv, s: lambda e: e.matmul(o_ap, KCT.r(s * 128, (s + 1) * 128)[kv * 64:(kv + 1) * 64],
                                                                      QT.r(sb * 512 + s * 32, sb * 512 + (s + 1) * 32)[kv * 64:(kv + 1) * 64],
                                                                      start=True, stop=True))(o_ap, kv, s),
                         reads=KCT.tk() + QT.tk(), writes=[BANK_t[bsp_[kv]]])
            for kv in range(2):
                pt = PT[kv]
                P.op("act", (lambda kv, pt: lambda e: e.activation(pt.r(), BANK[bsp_[kv]][:, :], AF.Exp, scale=0.125))(kv, pt),
                     reads=[BANK_t[bsp_[kv]]], writes=pt.tk())
                bfree(bsp_[kv])
                P.op("dve", (lambda pt: lambda e: e.tensor_tensor(out=pt.r().rearrange("p (a q) -> p a q", q=8), in0=pt.f().rearrange("p (a q) -> p a q", q=8),
                                                                    in1=mpast.unsqueeze(1).broadcast_to([128, 64, 8]), op=ALU.mult))(pt),
                     reads=pt.tk() + [con_t], writes=pt.tk())
            stopchk("s8a_pastscores_done")
            for kv in range(2):
                bs_ = balloc()
                pt = PT[2 + kv]
                P.op("pe", (lambda bs_, kv: lambda e: e.matmul(BANK[bs_][:, :],
                                                              KT.r(128 + c0, 256 + c0)[kv * 64:(kv + 1) * 64],
                                                              QT.r(sb * 512, (sb + 1) * 512)[kv * 64:(kv + 1) * 64], start=True, stop=True))(bs_, kv),
                     reads=KT.tk() + QT.tk(), writes=[BANK_t[bs_]])
                P.op("act", (lambda bs_, pt: lambda e: e.activation(pt.r(), BANK[bs_][:, :], AF.Exp, scale=0.125))(bs_, pt), reads=[BANK_t[bs_]], writes=pt.tk())
                bfree(bs_)
                P.op("dve", (lambda pt: lambda e: e.tensor_tensor(out=pt.r().rearrange("p (s g q) -> p s g q", g=4, q=8),
                                                                    in0=pt.f().rearrange("p (s g q) -> p s g q", g=4, q=8),
                                                                    in1=mnew.rearrange("p (s q) -> p s q", q=8).unsqueeze(2).broadcast_to([128, 16, 4, 8]), op=ALU.mult))(pt),
                     reads=pt.tk() + [con_t], writes=pt.tk())
            stopchk("s8_samplescores_done")
            bo_, bd_, bop, bdp = balloc(), balloc(), balloc(), balloc()
            for kv in range(2):
                P.op("pe", (lambda kv: lambda e: e.matmul(BANK[bo_][:, :], vzv[:, 1 + sb, kv, :], PT[2 + kv].r(), start=(kv == 0), stop=(kv == 1)))(kv),
                     reads=PT[2 + kv].tk() + VZ.tk(), writes=[BANK_t[bo_]])
                P.op("pe", (lambda kv: lambda e: e.matmul(BANK[bd_][:, :], oloR if kv == 0 else ohiR, PT[2 + kv].r(), start=(kv == 0), stop=(kv == 1)))(kv),
                     reads=PT[2 + kv].tk() + [con_t], writes=[BANK_t[bd_]])
            vzcv = VZC.r().rearrange("p (s a d) -> p s a d", s=8, a=2)
            for rnd in range(2):
                P.op("pool", lambda e: e.tensor_copy(VZC.r(), zcol.broadcast_to([128, 2048])), reads=[con_t], writes=VZC.tk())
                P.dma("vzc", (lambda rnd: lambda e: [e.dma_start(out=vzcv[:, :, kv, kv * 64:(kv + 1) * 64],
                                                                 in_=cv_d[l, rnd * 8:(rnd + 1) * 8, :, kv * 64:(kv + 1) * 64].rearrange("s k d -> k s d"))
                                                    for kv in range(2)])(rnd), writes=VZC.tk(), n=2, eng="pool")
                for s8 in range(8):
                    s = rnd * 8 + s8
                    for kv in range(2):
                        p_ap = PT[kv].r(s * 32, (s + 1) * 32)
                        P.op("pe", (lambda p_ap, s8, kv, s: lambda e: e.matmul(
                            BANK[bop][:, s * 32:(s + 1) * 32], vzcv[:, s8, kv, :], p_ap,
                            start=(kv == 0), stop=(kv == 1)))(p_ap, s8, kv, s),
                            reads=PT[kv].tk() + VZC.tk(), writes=[BANK_t[bop]])
                    for kv in range(2):
                        p_ap = PT[kv].r(s * 32, (s + 1) * 32)
                        P.op("pe", (lambda p_ap, kv, s: lambda e: e.matmul(
                            BANK[bdp][:, s * 32:(s + 1) * 32], oloR if kv == 0 else ohiR, p_ap,
                            start=(kv == 0), stop=(kv == 1)))(p_ap, kv, s),
                            reads=PT[kv].tk() + [con_t], writes=[BANK_t[bdp]])
            to_ = QN[1]
            P.op("act", lambda e: e.copy(to_.f(), BANK[bop][:, :]), reads=[BANK_t[bop]], writes=to_.tk())
            bfree(bop)
            td_ = R2
            P.op("act", lambda e: e.copy(td_.f(), BANK[bdp][:, :]), reads=[BANK_t[bdp]], writes=td_.tk())
            bfree(bdp)
            P.op("dve", lambda e: e.tensor_tensor(out=to_.f(), in0=to_.f(), in1=BANK[bo_][:, :], op=ALU.add), reads=to_.tk() + [BANK_t[bo_]], writes=to_.tk())
            P.op("dve", lambda e: e.tensor_tensor(out=td_.f(), in0=td_.f(), in1=BANK[bd_][:, :], op=ALU.add), reads=td_.tk() + [BANK_t[bd_]], writes=td_.tk())
            bfree(bo_)
            bfree(bd_)
            t_ = ntmpf()
            P.op("dve", lambda e: e.tensor_tensor(out=t_.f().rearrange("p (s g q) -> p s g q", g=4, q=8),
                                                  in0=td_.f().rearrange("p (s g q) -> p s g q", g=4, q=8),
                                                  in1=sinkexp[:, l].unsqueeze(1).unsqueeze(3).broadcast_to([128, 16, 4, 8]), op=ALU.add),
                 reads=td_.tk() + [sk_t], writes=t_.tk())
            P.op("act", lambda e: e.activation(t_.f(), t_.f(), AF.Ln), reads=t_.tk(), writes=t_.tk())
            P.op("act", lambda e: e.activation(t_.f(), t_.f(), AF.Exp, scale=-1.0), reads=t_.tk(), writes=t_.tk())
            P.op("dve", lambda e: e.tensor_tensor(out=onv[:, :, c0:c0 + 128].rearrange("p g (s q) -> p g s q", q=8),
                                                  in0=to_.f().rearrange("p (s g q) -> p g s q", g=4, q=8),
                                                  in1=t_.f().rearrange("p (s g q) -> p g s q", g=4, q=8), op=ALU.mult),
                 reads=to_.tk() + t_.tk(), writes=ON.tk())

        if DEBUG and l == 0 and ti == 0:
            for i_, bf_ in enumerate((SA, UB, ON, QT)):
                P.dma("dbg%d" % i_, (lambda i_, bf_: lambda e: e.dma_start(out=dbg_d[i_], in_=bf_.f()))(i_, bf_), reads=bf_.tk())
            P.dma("dbg4", lambda e: e.dma_start(out=dbg_d[4, :, 0:1024], in_=KT.f()), reads=KT.tk())
            P.dma("dbg5", lambda e: e.dma_start(out=dbg_d[5, :, 0:1280], in_=VZ.f()), reads=VZ.tk())
        stopchk("s9_attnsample_done")
        srcs = (SA, UB, ON)
        for j in range(8):
            wo, wot = wnext(1536)
            wov = wo[:, 0:1536].rearrange("p (b k c) -> p b k c", b=3, k=4)
            bys = []
            for br in range(3):
                by = balloc()
                bys.append(by)
                src = srcs[br]
                for kc in range(4):
                    P.op("pe", (lambda by, kc, br, wov, src: lambda e: e.matmul(BANK[by][:, 0:T], wov[:, br, kc, :], src.r(kc * 512, kc * 512 + T),
                                                                               start=(kc == 0), stop=(kc == 3)))(by, kc, br, wov, src),
                         reads=[wot] + src.tk(kc * 512, kc * 512 + T), writes=[BANK_t[by]])
            wg, wgt = wnext(3072)
            wgv = wg[:, 0:3072].rearrange("p (k b c) -> p k b c", k=8, b=3)
            for br in range(3):
                bg = balloc()
                by = bys[br]
                for k in range(8):
                    P.op("pe", (lambda bg, k, br, wgv: lambda e: e.matmul(BANK[bg][:, 0:T], wgv[:, k, br, :], hT[:, k, 0:T], start=(k == 0), stop=(k == 7)))(bg, k, br, wgv),
                         reads=[wgt, hT_t[k]], writes=[BANK_t[bg]])
                t = ntmpf()
                P.op("act", (lambda t, bg: lambda e: e.activation(t.f(0, T), BANK[bg][:, 0:T], AF.Sigmoid))(t, bg), reads=[BANK_t[bg]], writes=t.tk())
                bfree(bg)
                mj = (j * 512, j * 512 + T)
                if br == 0:
                    P.op("dve", (lambda t, by, mj: lambda e: e.tensor_tensor(out=MIX.r(*mj), in0=t.f(0, T), in1=BANK[by][:, 0:T], op=ALU.mult))(t, by, mj),
                         reads=t.tk() + [BANK_t[by]], writes=MIX.tk(*mj))
                else:
                    P.op("dve", (lambda t, by: lambda e: e.tensor_tensor(out=t.f(0, T), in0=t.f(0, T), in1=BANK[by][:, 0:T], op=ALU.mult))(t, by),
                         reads=t.tk() + [BANK_t[by]], writes=t.tk())
                    P.op("dve", (lambda t, mj: lambda e: e.tensor_tensor(out=MIX.r(*mj), in0=MIX.f(*mj), in1=t.f(0, T), op=ALU.add))(t, mj),
                         reads=t.tk() + MIX.tk(*mj), writes=MIX.tk(*mj))
                bfree(by)
        for dp in range(4):
            w, wt = wnext(2048)
            wv = w[:, 0:2048].rearrange("p (k c) -> p k c", k=8)
            for d in range(2):
                by = balloc()
                for k in range(8):
                    P.op("pe", (lambda by, k, d, wv: lambda e: e.matmul(BANK[by][:, 0:T], wv[:, k, d * 128:(d + 1) * 128], MIX.r(k * 512, k * 512 + T),
                                                                       start=(k == 0), stop=(k == 7)))(by, k, d, wv),
                         reads=[wt] + MIX.tk(k * 512, k * 512 + T), writes=[BANK_t[by]])
                kk = dp * 2 + d
                P.op("dve", (lambda by, kk: lambda e: e.tensor_tensor(out=xT[:, kk, 0:T], in0=BANK[by][:, 0:T], in1=xT[:, kk, 0:T], op=ALU.add))(by, kk),
                     reads=[BANK_t[by], xT_t[kk]], writes=[xT_t[kk]])
                bfree(by)


    stopped = False
    try:
        for ti, (b0, nb) in enumerate(tiles):
            T = nb * 128
            has_s = (b0 + nb == NBLK)
            Tp = T - 128 if has_s else T
            for b in range(nb):
                sg = STG[b % 2]
                r0 = (b0 + b) * 128
                P.dma("xin%d" % (b % 2), (lambda sg, r0: lambda e: e.dma_start(out=sg.f(), in_=xin_d[r0:r0 + 128, :]))(sg, r0), writes=sg.tk())
                for hh in range(2):
                    bt = balloc()
                    for k4 in range(4):
                        k = hh * 4 + k4
                        P.op("pe", (lambda bt, k4, k, sg: lambda e: e.transpose(BANK[bt][:, k4 * 128:(k4 + 1) * 128], sg.f(k * 128, (k + 1) * 128), ident))(bt, k4, k, sg),
                             reads=sg.tk() + [con_t], writes=[BANK_t[bt]])
                    P.op("act" if hh == 0 else "dve",
                         (lambda bt, hh, b: lambda e: (e.copy if hh == 0 else e.tensor_copy)(xT[:, hh * 4:(hh + 1) * 4, b * 128:(b + 1) * 128],
                                                                                             BANK[bt][:, :].rearrange("p (k t) -> p k t", k=4)))(bt, hh, b),
                         reads=[BANK_t[bt]], writes=xT_t[hh * 4:(hh + 1) * 4])
                    bfree(bt)
            for l in range(depth):
                rmsnorm(l, PC_N1, T)
                ffn(l, T)
                rmsnorm(l, PC_NM, T)
                mixer(l, ti, b0, nb, T, Tp, has_s)
                rmsnorm(l, PC_N2, T)
                ffn(l, T)
            for b in range(nb):
                sg = STG[b % 2]
                r0 = (b0 + b) * 128
                for hh in range(2):
                    bt = balloc()
                    for k4 in range(4):
                        k = hh * 4 + k4
                        P.op("pe", (lambda bt, k4, k, b: lambda e: e.transpose(BANK[bt][:, k4 * 128:(k4 + 1) * 128], xT[:, k, b * 128:(b + 1) * 128], ident))(bt, k4, k, b),
                             reads=[xT_t[k], con_t], writes=[BANK_t[bt]])
                    P.op("act" if hh == 0 else "dve",
                         (lambda bt, hh, sg: lambda e: (e.copy if hh == 0 else e.tensor_copy)(sg.f(hh * 512, (hh + 1) * 512), BANK[bt][:, :]))(bt, hh, sg),
                         reads=[BANK_t[bt]], writes=sg.tk(hh * 512, (hh + 1) * 512))
                    bfree(bt)
                P.dma("yout%d" % (b % 2), (lambda sg, r0: lambda e: e.dma_start(out=y_d[r0:r0 + 128, :], in_=sg.f()))(sg, r0), reads=sg.tk())
    except StopBuild:
        stopped = True
    assert stopped or wstate["next"] == len(wlist)
    P.emit()
    P.close()
    return nc


_CACHE = {}


def make_in_maps(inp):
    W = pack_weights(inp)
    par = pack_params(inp)
    con = make_consts()
    gsel = make_gsel()
    lnp = np.zeros((L, 4, 512), np.float32)
    ev = (np.arange(512) // 64) % 2 == 0
    for l in range(L):
        lnp[l, 0, ev] = inp["ln_gmlp_g"][l][ev]
        lnp[l, 1, ev] = inp["ln_gmlp_b"][l][ev]
        lnp[l, 2, ~ev] = inp["ln_gmlp_g"][l][~ev]
        lnp[l, 3, ~ev] = inp["ln_gmlp_b"][l][~ev]
    wsT = np.ascontiguousarray(inp["w_spatial"].transpose(0, 3, 1, 2))
    wsS = np.zeros((L, 128, 8, 128), np.float32)
    for s in range(16):
        wsS[:, s * 8:(s + 1) * 8, :, s * 8:(s + 1) * 8] = inp["w_spatial"][:, :, :8, :8].transpose(0, 3, 1, 2)
    bsp = np.zeros((L, 128, 256), np.float32)
    bsp[:, 0:8, 0:128] = inp["b_spatial"]
    bsp[:, 0:8, 128:256] = np.tile(inp["b_spatial"][:, :, :8], (1, 1, 16))
    in_maps = []
    for c in range(NCORE):
        b, half = c // 2, c % 2
        start = 0 if half == 0 else B_START
        cs, sn = rope_tables(start)
        xin = np.concatenate([inp["x_prompt"][b, start:start + A_LEN], inp["x_sample"][16 * c:16 * c + 16].reshape(128, D)], axis=0)
        in_maps.append({
            "xin": np.ascontiguousarray(xin), "wl": W, "par": par, "con": con, "gsel": gsel, "cos": cs, "sin": sn,
            "lnp": lnp, "wsT": wsT, "wsS": wsS, "bsp": bsp,
            "stc": np.ascontiguousarray(inp["state_conv"][:, 16 * c:16 * c + 16]),
            "ck": np.ascontiguousarray(inp["cache_k"][:, 16 * c:16 * c + 16].reshape(L, 16, 128, 128)),
            "cv": np.ascontiguousarray(inp["cache_v"][:, 16 * c:16 * c + 16].reshape(L, 16, 128, 128)),
        })
    return in_maps


def kernel(**inputs):
    inp = {k: np.asarray(v) for k, v in inputs.items()}
    if "nc" not in _CACHE:
        _CACHE["nc"] = build()
    nc = _CACHE["nc"]
    in_maps = make_in_maps(inp)
    res = run_bass_kernel_spmd(nc, in_maps, core_ids=list(range(NCORE)))
    R = res.results
    y_prompt = np.empty((4, 4096, D), np.float32)
    y_sample = np.empty((128, 8, D), np.float32)
    conv_p = np.empty((L, 4, 30, 512), np.float32)
    conv_s = np.empty((L, 128, 30, 512), np.float32)
    k_p = np.empty((L, 4, 128, 2, 64), np.float32)
    v_p = np.empty((L, 4, 128, 2, 64), np.float32)
    k_s = np.empty((L, 128, 8, 2, 64), np.float32)
    v_s = np.empty((L, 128, 8, 2, 64), np.float32)
    gv_s = np.empty((L, 128, 8, 512), np.float32)
    for c in range(NCORE):
        b, half = c // 2, c % 2
        r = R[c]
        y = r["y"]
        if half == 0:
            y_prompt[b, 0:A_LEN] = y[0:A_LEN]
        else:
            y_prompt[b, A_LEN:4096] = y[A_LEN - (4096 - A_LEN):A_LEN]
            conv_p[:, b] = r["convp"]
            k_p[:, b] = r["kp"].reshape(L, 128, 2, 64)
            v_p[:, b] = r["vp"].reshape(L, 128, 2, 64)
        y_sample[16 * c:16 * c + 16] = y[A_LEN:].reshape(16, 8, D)
        conv_s[:, 16 * c:16 * c + 16] = r["convs"]
        k_s[:, 16 * c:16 * c + 16] = r["ks"].reshape(L, 16, 8, 2, 64)
        v_s[:, 16 * c:16 * c + 16] = r["vs"].reshape(L, 16, 8, 2, 64)
        gv_s[:, 16 * c:16 * c + 16] = r["gv"].reshape(L, 16, 8, 512)
    return (y_prompt, y_sample, conv_p, conv_s, k_p, v_p, k_s, v_s, gv_s)
```
